# Optimizing a Trainium2 kernel written in Bass

```python
import math
import jax, jax.numpy as jnp
from jax import lax
import numpy as np

D_MODEL = 1024
BATCH = 4
SEQ = 4096
DEPTH = 4
DEC_BATCH = 128
DEC_SEQ = 8
PAST_LEN = 2048
PAGE_SIZE = 128

PLE_DIM = 256
D_FF = 4 * D_MODEL
RMS_EPS = 1e-6
ROPE_THETA = 10000.0
N_HEADS = 16
HEAD_DIM = D_MODEL // N_HEADS
KV_HEADS = 4
Q_PER_KV = N_HEADS // KV_HEADS
Q_DIM = N_HEADS * HEAD_DIM
KV_DIM = KV_HEADS * HEAD_DIM
N_BRANCH = 3
CMP_BLOCK = 32
CMP_HIDDEN = 4 * HEAD_DIM
SEL_BLOCK = 64
CMP_PER_SEL = SEL_BLOCK // CMP_BLOCK
TOP_N = 16
WINDOW = 512
Q_BLOCK = 128
NSA_IN = Q_DIM + 6 * KV_DIM + N_BRANCH * N_HEADS
FORCED_CUR = 3e4
FORCED_FIRST = 2e4
HG_EXPAND = 128
HG_HEADS = D_MODEL // HG_EXPAND
HG_DK = HG_EXPAND
HG_DV = D_MODEL // HG_HEADS
HG_CHUNK = 64
HG_IN = 4 * D_MODEL
N_ATTN_LAYERS = (DEPTH + 1) // 2
N_REC_LAYERS = DEPTH // 2

kernel_name = 'nsa_hgrn2_hybrid_step'


def rmsnorm(x, g):
    xf = x.astype(jnp.float32)
    y = xf * lax.rsqrt(jnp.mean(xf * xf, axis=-1, keepdims=True) + RMS_EPS)
    return (y * g.astype(jnp.float32)).astype(x.dtype)


def rope(x, pos):
    half = HEAD_DIM // 2
    inv = ROPE_THETA ** (-jnp.arange(half, dtype=jnp.float32) / half)
    ang = pos.astype(jnp.float32)[:, None] * inv[None, :]
    cos = jnp.cos(ang)[:, None, :]
    sin = jnp.sin(ang)[:, None, :]
    xf = x.astype(jnp.float32)
    x1, x2 = xf[..., :half], xf[..., half:]
    return jnp.concatenate([x1 * cos - x2 * sin, x1 * sin + x2 * cos], axis=-1).astype(x.dtype)


def masked_softmax(s, mask):
    s = jnp.where(mask, s.astype(jnp.float32), -1e30)
    m = jnp.max(s, axis=-1, keepdims=True)
    e = jnp.where(mask, jnp.exp(s - m), 0.0)
    return e / jnp.maximum(jnp.sum(e, axis=-1, keepdims=True), 1e-30)


def compress(kb, pe, w1, w2):
    h = jnp.einsum('nlgd,ldh->ngh', kb + pe[None, :, None, :], w1)
    return jnp.einsum('ngh,hd->ngd', jax.nn.gelu(h), w2)


def nsa_one(q, kv, kw, vw, gates, q_pos0, win_pos0, pe, w1, w2):
    Tq = q.shape[0]
    L = kv.shape[0]
    L_pad = -(-L // SEL_BLOCK) * SEL_BLOCK
    kv = jnp.pad(kv, ((0, L_pad - L), (0, 0), (0, 0), (0, 0)))
    n_cmp = L_pad // CMP_BLOCK
    n_sel = L_pad // SEL_BLOCK
    kc = compress(kv[:, 0].reshape(n_cmp, CMP_BLOCK, KV_HEADS, HEAD_DIM), pe[0], w1[0], w2[0])
    vc = compress(kv[:, 1].reshape(n_cmp, CMP_BLOCK, KV_HEADS, HEAD_DIM), pe[1], w1[1], w2[1])
    ks = kv[:, 2].reshape(n_sel, SEL_BLOCK, KV_HEADS, HEAD_DIM).transpose(2, 0, 1, 3)
    vs = kv[:, 3].reshape(n_sel, SEL_BLOCK, KV_HEADS, HEAD_DIM).transpose(2, 0, 1, 3)
    kw_pad = jnp.pad(kw, ((WINDOW, 0), (0, 0), (0, 0)))
    vw_pad = jnp.pad(vw, ((WINDOW, 0), (0, 0), (0, 0)))
    n_top = min(TOP_N, n_sel)
    qb = math.gcd(Tq, Q_BLOCK)
    scale = HEAD_DIM ** -0.5
    cmp_end = (jnp.arange(n_cmp, dtype=jnp.int32) + 1) * CMP_BLOCK - 1
    blk = jnp.arange(n_sel, dtype=jnp.int32)
    g_idx = jnp.arange(KV_HEADS)[None, :, None]

    def block(i):
        s = i * qb
        qi = lax.dynamic_slice_in_dim(q, s, qb, 0).reshape(qb, KV_HEADS, Q_PER_KV, HEAD_DIM)
        gi = lax.dynamic_slice_in_dim(gates, s, qb, 0).reshape(qb, KV_HEADS, Q_PER_KV, N_BRANCH)
        t = q_pos0 + s + jnp.arange(qb, dtype=jnp.int32)
        sc = jnp.einsum('qgrd,ngd->qgrn', qi, kc) * scale
        pc = masked_softmax(sc, (cmp_end[None, :] <= t[:, None])[:, None, None, :])
        o_c = jnp.einsum('qgrn,ngd->qgrd', pc.astype(vc.dtype), vc)
        imp = pc.sum(2).reshape(qb, KV_HEADS, n_sel, CMP_PER_SEL).sum(-1)
        cur = (t // SEL_BLOCK)[:, None, None]
        imp = jnp.where(blk[None, None, :] > cur, -1.0, imp)
        imp = jnp.where(blk[None, None, :] == 0, FORCED_FIRST, imp)
        imp = jnp.where(blk[None, None, :] == cur, FORCED_CUR, imp)
        _, idx = lax.top_k(imp, n_top)
        ksel = ks[g_idx, idx].reshape(qb, KV_HEADS, n_top * SEL_BLOCK, HEAD_DIM)
        vsel = vs[g_idx, idx].reshape(qb, KV_HEADS, n_top * SEL_BLOCK, HEAD_DIM)
        kpos = (idx[..., None] * SEL_BLOCK + jnp.arange(SEL_BLOCK, dtype=jnp.int32)).reshape(qb, KV_HEADS, 1, -1)
        ss = jnp.einsum('qgrd,qgmd->qgrm', qi, ksel) * scale
        ps = masked_softmax(ss, kpos <= t[:, None, None, None])
        o_s = jnp.einsum('qgrm,qgmd->qgrd', ps.astype(vsel.dtype), vsel)
        j0 = q_pos0 + s - win_pos0
        kwi = lax.dynamic_slice_in_dim(kw_pad, j0, WINDOW + qb, 0)
        vwi = lax.dynamic_slice_in_dim(vw_pad, j0, WINDOW + qb, 0)
        wpos = q_pos0 + s - WINDOW + jnp.arange(WINDOW + qb, dtype=jnp.int32)
        mw = (wpos[None, :] >= win_pos0) & (wpos[None, :] <= t[:, None]) & (wpos[None, :] > t[:, None] - WINDOW)
        sw = jnp.einsum('qgrd,kgd->qgrk', qi, kwi) * scale
        pw = masked_softmax(sw, mw[:, None, None, :])
        o_w = jnp.einsum('qgrk,kgd->qgrd', pw.astype(vwi.dtype), vwi)
        return gi[..., 0:1] * o_c + gi[..., 1:2] * o_s + gi[..., 2:3] * o_w

    o = lax.map(block, jnp.arange(Tq // qb))
    return o.reshape(Tq, N_HEADS, HEAD_DIM)


def nsa_project(h, pos, w_in):
    B, T, _ = h.shape
    q, kv, gl = jnp.split(h @ w_in, [Q_DIM, Q_DIM + 6 * KV_DIM], axis=-1)
    q = rope(q.reshape(B, T, N_HEADS, HEAD_DIM), pos)
    kv = kv.reshape(B, T, 3, 2, KV_HEADS, HEAD_DIM)
    k = rope(kv[:, :, :, 0].reshape(B, T, 3 * KV_HEADS, HEAD_DIM), pos).reshape(B, T, 3, KV_HEADS, HEAD_DIM)
    kv = jnp.stack([k, kv[:, :, :, 1]], axis=3).reshape(B, T, 6, KV_HEADS, HEAD_DIM)
    gates = jax.nn.sigmoid(gl.astype(jnp.float32)).astype(h.dtype).reshape(B, T, N_HEADS, N_BRANCH)
    return q, kv, gates


def nsa_prompt(h, pos, w_in, pe, w1, w2, w_out):
    B, T, _ = h.shape
    q, kv, gates = nsa_project(h, pos, w_in)
    rows, wrows = kv[:, :, :4], kv[:, :, 4:]

    def one(args):
        q_i, r_i, w_i, g_i = args
        return nsa_one(q_i, r_i, w_i[:, 0], w_i[:, 1], g_i, 0, 0, pe, w1, w2)

    o = lax.map(one, (q, rows, wrows, gates))
    y = o.reshape(B, T, D_MODEL) @ w_out
    return y, rows, wrows[:, T - min(WINDOW, T):]


def nsa_sample(h, pos, pool, win_buf, page_table, past_len, w_in, pe, w1, w2, w_out):
    B, T, _ = h.shape
    q, kv, gates = nsa_project(h, pos, w_in)
    rows, wrows = kv[:, :, :4], kv[:, :, 4:]
    n_buf = win_buf.shape[1]

    def one(args):
        q_i, r_i, w_i, g_i, pt_i, buf_i = args
        past = pool[pt_i].reshape(past_len, 4, KV_HEADS, HEAD_DIM)
        kv_i = jnp.concatenate([past, r_i], axis=0)
        wk = jnp.concatenate([buf_i, w_i], axis=0)
        return nsa_one(q_i, kv_i, wk[:, 0], wk[:, 1], g_i, past_len, past_len - n_buf, pe, w1, w2)

    o = lax.map(one, (q, rows, wrows, gates, page_table, win_buf))
    y = o.reshape(B, T, D_MODEL) @ w_out
    new_buf = jnp.concatenate([win_buf, wrows], axis=1)[:, T:]
    return y, rows, new_buf


def hgrn_scan(q, k, v, log_f, S0):
    B, T = q.shape[:2]
    C = math.gcd(T, HG_CHUNK)
    n = T // C

    def resh(a):
        return a.reshape(B, n, C, HG_HEADS, a.shape[-1]).transpose(1, 0, 3, 2, 4)

    tri = jnp.tril(jnp.ones((C, C), dtype=bool))[:, :, None]

    def step(S, xs):
        qc, kc, vc, fc = xs
        b = jnp.cumsum(fc, axis=2)
        o_inter = jnp.einsum('bhtk,bhkv->bhtv', qc * jnp.exp(b), S)
        diff = b[:, :, :, None, :] - b[:, :, None, :, :]
        dec = jnp.where(tri, jnp.exp(jnp.where(tri, diff, 0.0)), 0.0)
        A = jnp.einsum('bhtk,bhsk,bhtsk->bhts', qc, kc, dec)
        o_intra = jnp.einsum('bhts,bhsv->bhtv', A, vc)
        b_last = b[:, :, -1]
        S_new = jnp.exp(b_last)[..., None] * S + jnp.einsum('bhsk,bhsv->bhkv', kc * jnp.exp(b_last[:, :, None] - b), vc)
        return S_new, o_inter + o_intra

    S, o = lax.scan(step, S0, (resh(q), resh(k), resh(v), resh(log_f)))
    return o.transpose(1, 0, 3, 2, 4).reshape(B, T, HG_HEADS, HG_DV), S


def hgrn_mixer(h, S0, w_in, lb, norm_g, w_out):
    B, T, _ = h.shape
    f32 = jnp.float32
    q, f, i, g = jnp.split((h @ w_in).astype(f32), 4, axis=-1)
    shp = (B, T, HG_HEADS, HG_DK)
    q = (jax.nn.silu(q) * HG_DK ** -0.5).reshape(shp)
    lbf = lb.astype(f32)
    log_f = jnp.log(lbf + (1.0 - lbf) * jax.nn.sigmoid(f)).reshape(shp)
    k = ((1.0 - lbf) * jax.nn.sigmoid(-f)).reshape(shp)
    v = i.reshape(B, T, HG_HEADS, HG_DV)
    o, S = hgrn_scan(q, k, v, log_f, S0.astype(f32))
    o = rmsnorm(o, norm_g) * jax.nn.silu(g).reshape(B, T, HG_HEADS, HG_DV)
    y = o.reshape(B, T, D_MODEL).astype(h.dtype) @ w_out
    return y, S.astype(S0.dtype)


def post_block(x, p_i, g_mlp, w1, w2, g_ple, w_gate, w_proj):
    hid = jax.nn.relu(rmsnorm(x, g_mlp) @ w1)
    x = x + (hid * hid) @ w2
    gate = jax.nn.sigmoid((rmsnorm(x, g_ple) @ w_gate).astype(jnp.float32)).astype(x.dtype)
    return x + (p_i @ w_proj) * gate


def setup_inputs(seed: int = 0) -> dict:
    key = jax.random.key(seed)
    ks = jax.random.split(key, 32)
    f32 = jnp.float32

    def nrm(k, shape, scale):
        return jax.random.normal(k, shape, f32) * scale

    def gain(k, shape):
        return 1.0 + 0.01 * jax.random.normal(k, shape, f32)

    n_pages = PAST_LEN // PAGE_SIZE
    n_used = DEC_BATCH * n_pages
    n_phys = (n_used * 5 + 3) // 4
    page_table = jax.random.permutation(ks[5], n_phys)[:n_used].astype(jnp.int32).reshape(DEC_BATCH, n_pages)
    win_buf = min(WINDOW, PAST_LEN)
    return {
        'x_prompt': nrm(ks[0], (BATCH, SEQ, D_MODEL), 1.0),
        'x_sample': nrm(ks[1], (DEC_BATCH, DEC_SEQ, D_MODEL), 1.0),
        'cache_nsa_kv': nrm(ks[2], (N_ATTN_LAYERS, n_phys, PAGE_SIZE, 4, KV_HEADS, HEAD_DIM), 1.0),
        'state_nsa_win': nrm(ks[3], (N_ATTN_LAYERS, DEC_BATCH, win_buf, 2, KV_HEADS, HEAD_DIM), 1.0),
        'state_hgrn': nrm(ks[4], (N_REC_LAYERS, DEC_BATCH, HG_HEADS, HG_DK, HG_DV), 0.5),
        'page_table': page_table,
        'p_prompt': nrm(ks[6], (DEPTH, BATCH, SEQ, PLE_DIM), 1.0),
        'p_sample': nrm(ks[7], (DEPTH, DEC_BATCH, DEC_SEQ, PLE_DIM), 1.0),
        'norm_mix': gain(ks[8], (DEPTH, D_MODEL)),
        'norm_mlp': gain(ks[9], (DEPTH, D_MODEL)),
        'norm_ple': gain(ks[10], (DEPTH, D_MODEL)),
        'norm_final': gain(ks[11], (D_MODEL,)),
        'nsa_w_in': nrm(ks[12], (N_ATTN_LAYERS, D_MODEL, NSA_IN), D_MODEL ** -0.5),
        'nsa_cmp_pe': nrm(ks[13], (N_ATTN_LAYERS, 2, CMP_BLOCK, HEAD_DIM), 0.1),
        'nsa_cmp_w1': nrm(ks[14], (N_ATTN_LAYERS, 2, CMP_BLOCK, HEAD_DIM, CMP_HIDDEN), (CMP_BLOCK * HEAD_DIM) ** -0.5),
        'nsa_cmp_w2': nrm(ks[15], (N_ATTN_LAYERS, 2, CMP_HIDDEN, HEAD_DIM), CMP_HIDDEN ** -0.5),
        'nsa_w_out': nrm(ks[16], (N_ATTN_LAYERS, D_MODEL, D_MODEL), D_MODEL ** -0.5),
        'hg_w_in': nrm(ks[17], (N_REC_LAYERS, D_MODEL, HG_IN), D_MODEL ** -0.5),
        'hg_lb_logits': nrm(ks[18], (N_REC_LAYERS, D_MODEL), 0.5),
        'hg_norm': gain(ks[19], (N_REC_LAYERS, HG_DV)),
        'hg_w_out': nrm(ks[20], (N_REC_LAYERS, D_MODEL, D_MODEL), D_MODEL ** -0.5),
        'mlp_w1': nrm(ks[21], (DEPTH, D_MODEL, D_FF), D_MODEL ** -0.5),
        'mlp_w2': nrm(ks[22], (DEPTH, D_FF, D_MODEL), D_FF ** -0.5),
        'ple_w_proj': nrm(ks[23], (DEPTH, PLE_DIM, D_MODEL), PLE_DIM ** -0.5),
        'ple_w_gate': nrm(ks[24], (DEPTH, D_MODEL, D_MODEL), D_MODEL ** -0.5),
    }


def reference(x_prompt, x_sample, cache_nsa_kv, state_nsa_win, state_hgrn, page_table, p_prompt, p_sample,
              norm_mix, norm_mlp, norm_ple, norm_final,
              nsa_w_in, nsa_cmp_pe, nsa_cmp_w1, nsa_cmp_w2, nsa_w_out,
              hg_w_in, hg_lb_logits, hg_norm, hg_w_out,
              mlp_w1, mlp_w2, ple_w_proj, ple_w_gate):
    T_p = x_prompt.shape[1]
    T_s = x_sample.shape[1]
    past_len = page_table.shape[1] * cache_nsa_kv.shape[2]
    pos_p = jnp.arange(T_p, dtype=jnp.int32)
    pos_s = past_len + jnp.arange(T_s, dtype=jnp.int32)
    lb_w = jax.nn.softmax(hg_lb_logits.astype(jnp.float32), axis=0)
    lower_bounds = jnp.cumsum(lb_w, axis=0) - lb_w[0:1]

    xp, xs = x_prompt, x_sample
    kv_p, kv_s, win_p, win_s, st_p, st_s = [], [], [], [], [], []
    for i in range(DEPTH):
        hp = rmsnorm(xp, norm_mix[i])
        hs = rmsnorm(xs, norm_mix[i])
        if i % 2 == 0:
            a = i // 2
            yp, rp, wp = nsa_prompt(hp, pos_p, nsa_w_in[a], nsa_cmp_pe[a], nsa_cmp_w1[a], nsa_cmp_w2[a], nsa_w_out[a])
            ys, rs, ws = nsa_sample(hs, pos_s, cache_nsa_kv[a], state_nsa_win[a], page_table, past_len,
                                    nsa_w_in[a], nsa_cmp_pe[a], nsa_cmp_w1[a], nsa_cmp_w2[a], nsa_w_out[a])
            kv_p.append(rp)
            kv_s.append(rs)
            win_p.append(wp)
            win_s.append(ws)
        else:
            r = i // 2
            S0 = jnp.zeros((xp.shape[0], HG_HEADS, HG_DK, HG_DV), dtype=state_hgrn.dtype)
            yp, sp = hgrn_mixer(hp, S0, hg_w_in[r], lower_bounds[r], hg_norm[r], hg_w_out[r])
            ys, ss = hgrn_mixer(hs, state_hgrn[r], hg_w_in[r], lower_bounds[r], hg_norm[r], hg_w_out[r])
            st_p.append(sp)
            st_s.append(ss)
        xp = post_block(xp + yp, p_prompt[i], norm_mlp[i], mlp_w1[i], mlp_w2[i], norm_ple[i], ple_w_gate[i], ple_w_proj[i])
        xs = post_block(xs + ys, p_sample[i], norm_mlp[i], mlp_w1[i], mlp_w2[i], norm_ple[i], ple_w_gate[i], ple_w_proj[i])
    y_prompt = rmsnorm(xp, norm_final)
    y_sample = rmsnorm(xs, norm_final)
    return (y_prompt, y_sample, jnp.stack(kv_p), jnp.stack(kv_s), jnp.stack(win_p), jnp.stack(win_s),
            jnp.stack(st_p), jnp.stack(st_s))
```

```python
import math
from contextlib import ExitStack
import numpy as np
import ml_dtypes
import concourse.bass as bass
import concourse.mybir as mybir
from concourse.bass_utils import run_bass_kernel_spmd

F32 = mybir.dt.float32
BF16 = mybir.dt.bfloat16
I32 = mybir.dt.int32
AF = mybir.ActivationFunctionType
ALU = mybir.AluOpType
AX = mybir.AxisListType

D = 1024
NH = 16
HD = 64
G = 4
PLE = 256
DFF = 4096
NSA_IN = 2608
NEG = -30000.0
SCALE = HD ** -0.5


class Sem:
    __slots__ = ("h", "total", "is_dma", "name", "owner")

    def __init__(self, h, is_dma, name):
        self.h = h
        self.total = 0
        self.is_dma = is_dma
        self.name = name
        self.owner = None


class Trk:
    __slots__ = ("w", "rs", "dsem", "name")

    def __init__(self, name=""):
        self.w = None
        self.rs = {}
        self.dsem = None
        self.name = name


class T:
    def __init__(self, t, name):
        self.t = t
        self.k = Trk(name)


def _trk(x):
    return x.k if isinstance(x, T) else x


class Eng:
    def __init__(self, fw, name, h):
        self.name = name
        self.h = h
        self.sem = fw.new_sem(False, "e_" + name)
        self.sem.owner = self
        self.known = {}
        self.hist_idx = []
        self.hist_snap = []

    def _merge_from(self, other, val):
        import bisect
        i = bisect.bisect_right(other.hist_idx, val) - 1
        if i >= 0:
            for k, v in other.hist_snap[i].items():
                if self.known.get(k, 0) < v:
                    self.known[k] = v

    def wait_for(self, deps):
        changed = False
        for sem, val in deps:
            if sem.is_dma:
                val = sem.total
            if self.known.get(sem, 0) < val:
                self.h.wait_ge(sem.h, val)
                self.known[sem] = val
                changed = True
                own = getattr(sem, "owner", None)
                if own is not None and own is not self:
                    self._merge_from(own, val)
        if changed:
            nxt = self.sem.total + 1
            if self.hist_idx and self.hist_idx[-1] == nxt:
                self.hist_snap[-1] = dict(self.known)
            else:
                self.hist_idx.append(nxt)
                self.hist_snap.append(dict(self.known))


class FW:
    def __init__(self, nc, stack):
        self.nc = nc
        self.stack = stack
        self.sems = []
        self.n_ins = 0
        self.free_dsems = []
        self.pe = Eng(self, "pe", nc.tensor)
        self.act = Eng(self, "act", nc.scalar)
        self.dve = Eng(self, "dve", nc.vector)
        self.pool = Eng(self, "pool", nc.gpsimd)
        self.sp = Eng(self, "sp", nc.sync)
        self.dd_sem = self.new_sem(True, "dd")

    def new_sem(self, is_dma, name):
        h = self.stack.enter_context(self.nc.semaphore(name))
        s = Sem(h, is_dma, name)
        self.sems.append(s)
        return s

    def _deps(self, eng, reads, writes):
        deps = []
        for t in reads:
            t = _trk(t)
            if t.w is not None:
                deps.append(t.w)
        for t in writes:
            t = _trk(t)
            if t.w is not None:
                deps.append(t.w)
            for s, v in t.rs.items():
                deps.append((s, v))
        if eng is self.pe:
            deps = [d for d in deps if d[0] is not eng.sem]
        return deps

    def op(self, eng, fn, reads=(), writes=()):
        eng.wait_for(self._deps(eng, reads, writes))
        ins = fn()
        sem = eng.sem
        sem.total += 1
        ins.then_inc(sem.h, 1)
        ev = (sem, sem.total)
        for t in reads:
            _trk(t).rs[sem] = sem.total
        for t in writes:
            t = _trk(t)
            t.w = ev
            t.rs = {}
        self.n_ins += 1
        return ins

    def dma(self, q, fn, reads=(), writes=(), sb=None):
        q.wait_for(self._deps(None, reads, writes))
        if sb is None:
            sem = self.dd_sem
        else:
            sb = _trk(sb)
            if sb.dsem is None:
                sb.dsem = self.free_dsems.pop() if self.free_dsems else self.new_sem(True, "d_" + sb.name)
            sem = sb.dsem
        ins = fn()
        sem.total += 16
        ins.then_inc(sem.h, 16)
        ev = (sem, sem.total)
        for t in reads:
            _trk(t).rs[sem] = sem.total
        for t in writes:
            t = _trk(t)
            t.w = ev
            t.rs = {}
        self.n_ins += 1
        return ins

    def release(self, tiles):
        for t in tiles:
            k = _trk(t)
            if k.dsem is not None:
                self.free_dsems.append(k.dsem)
                k.dsem = None

    def barrier(self):
        alld = [(s, s.total) for s in self.sems if s.total > 0]
        for e in (self.pe, self.act, self.dve, self.pool, self.sp):
            e.wait_for(alld)

    def finish(self):
        alld = [(s, s.total) for s in self.sems if s.total > 0]
        self.sp.wait_for(alld)


class Cfg:
    def __init__(self, SEQ=4096, NS=16, NPG=16, NPHYS=2560, depth=4):
        self.SEQ = SEQ
        self.NS = NS
        self.TS = 8
        self.NPG = NPG
        self.NPHYS = NPHYS
        self.PAST = NPG * 128
        self.WIN = 512
        self.NT = SEQ // 128
        self.NTT = self.NT + 1
        self.depth = depth
        assert NS * 8 == 128


def host_consts(cfg):
    bf = ml_dtypes.bfloat16
    c = {}
    half = HD // 2
    inv = (10000.0 ** (-np.arange(half, dtype=np.float32) / half)).astype(np.float32)
    pos = np.concatenate([np.arange(cfg.SEQ), np.tile(cfg.PAST + np.arange(8), cfg.NS)]).astype(np.float32)
    ang = pos[:, None] * inv[None, :]
    c["ropec"] = np.cos(ang).astype(np.float32)
    c["ropes"] = np.sin(ang).astype(np.float32)
    c["identb"] = np.eye(128).astype(bf)
    c["identf"] = np.eye(128).astype(np.float32)
    c["I4"] = np.tile(np.eye(128), (1, 4)).astype(bf)
    q = np.arange(128)
    m = np.arange(256) - 128
    c["BcT"] = np.where(m[None, :] < ((q[:, None] + 1) // 32), 0.0, NEG).astype(bf)
    kl = np.arange(128)
    c["BcauT"] = np.where(kl[None, :] <= q[:, None], 0.0, NEG).astype(bf)
    c["BwinT"] = np.where(kl[None, :] > q[:, None], 0.0, NEG).astype(bf)
    mm = np.arange(128) - 64
    cur = (q >= 64).astype(np.int64)
    keep = np.ones((128, 128), np.float32)
    forced = np.zeros((128, 128), np.float32)
    fut = mm[None, :] > cur[:, None]
    eq = mm[None, :] == cur[:, None]
    keep[fut | eq] = 0.0
    forced[fut] = -1.0
    forced[eq] = 3e4
    c["keep0"] = keep
    c["forced0"] = forced
    n = np.arange(128)
    c["pairc"] = (n[:, None] // 2 == np.arange(64)[None, :]).astype(bf)
    c["iota_p"] = np.arange(128, dtype=np.float32).reshape(128, 1)
    t8 = np.arange(8)
    c["I8x4"] = np.tile(np.eye(8), (1, 4)).astype(bf)
    u = np.arange(256) - 128
    c["Bnew0"] = np.where((u[None, :] >= 0) & (u[None, :] <= t8[:, None]), 0.0, NEG).astype(bf)
    c["BwinS"] = np.where(kl[None, :] > t8[:, None], 0.0, NEG).astype(bf)
    c["pairS"] = (np.arange(64)[:, None] // 2 == np.arange(32)[None, :]).astype(bf)
    s64 = np.arange(64)
    c["maskA"] = (s64[:, None] <= s64[None, :]).astype(np.float32)
    r = np.arange(128)
    c["maskS"] = ((r[:, None] // 8 == r[None, :] // 8) & (r[:, None] % 8 <= r[None, :] % 8)).astype(np.float32)
    c["rowsel"] = (r[:, None] // 8 == np.arange(16)[None, :]).astype(np.float32)
    return c


CONST_SHAPES = None


class CD:
    def __init__(self, din):
        self.din = din

    def __getitem__(self, k):
        return self.din["c_" + k]


class LazyIn(dict):
    def __init__(self, nc):
        super().__init__()
        self.nc = nc
        self.spec = {}

    def __missing__(self, name):
        shape, dt = self.spec[name]
        ap = self.nc.dram_tensor(name, shape, dt, kind="ExternalInput").ap()
        self[name] = ap
        return ap


class Prog:
    def __init__(self, cfg, stages=("nsa", "hg", "post")):
        self.cfg = cfg
        self.stages = stages
        self.nc = bass.Bass("TRN2", target_bir_lowering=False)
        self.din = LazyIn(self.nc)
        self.dout = {}

    def inp(self, name, shape, dt=F32):
        self.din.spec[name] = (list(shape), dt)
        return None

    def outp(self, name, shape, dt=F32):
        self.dout[name] = self.nc.dram_tensor(name, list(shape), dt, kind="ExternalOutput").ap()
        return self.dout[name]

    def sb(self, st, name, shape, dt):
        self.uid = getattr(self, "uid", 0) + 1
        name = "%s_%d" % (name, self.uid)
        return T(st.enter_context(self.nc.sbuf_tensor(name, list(shape), dt)), name)

    def mm(self, out, lhsT, rhs, start, stop, R, W, skip=False):
        nc = self.nc
        return self.f.op(self.f.pe, lambda: nc.tensor.matmul(out, lhsT=lhsT, rhs=rhs, start=start, stop=stop, skip_group_check=skip), R, W)

    def tr(self, out, in_, ident, R, W):
        nc = self.nc
        return self.f.op(self.f.pe, lambda: nc.tensor.transpose(out=out, in_=in_, identity=ident), R, W)

    def act(self, out, in_, func, R, W, bias=None, scale=None, accum_out=None):
        nc = self.nc
        kw = {}
        if bias is not None:
            kw["bias"] = bias
        if scale is not None:
            kw["scale"] = scale
        if accum_out is not None:
            kw["accum_out"] = accum_out
        return self.f.op(self.f.act, lambda: nc.scalar.activation(out=out, in_=in_, func=func, **kw), R, W)

    def _ve(self, eng):
        if eng is None or eng == "dve":
            return self.f.dve, self.nc.vector
        return self.f.pool, self.nc.gpsimd

    def tt(self, out, a, b, op, R, W, eng=None):
        e, h = self._ve(eng)
        return self.f.op(e, lambda: h.tensor_tensor(out=out, in0=a, in1=b, op=op), R, W)

    def ts(self, out, a, s1, s2, op0, op1, R, W, eng=None):
        e, h = self._ve(eng)
        if op1 is None:
            return self.f.op(e, lambda: h.tensor_scalar(out=out, in0=a, scalar1=s1, scalar2=None, op0=op0), R, W)
        return self.f.op(e, lambda: h.tensor_scalar(out=out, in0=a, scalar1=s1, scalar2=s2, op0=op0, op1=op1), R, W)

    def stt(self, out, a, s, b, op0, op1, R, W):
        nc = self.nc
        return self.f.op(self.f.dve, lambda: nc.vector.scalar_tensor_tensor(out=out, in0=a, scalar=s, in1=b, op0=op0, op1=op1), R, W)

    def cp(self, out, in_, R, W, eng=None):
        if eng == "act":
            nc = self.nc
            return self.f.op(self.f.act, lambda: nc.scalar.copy(out=out, in_=in_), R, W)
        e, h = self._ve(eng)
        return self.f.op(e, lambda: h.tensor_copy(out=out, in_=in_), R, W)

    def mset(self, ap, val, W, eng="pool"):
        e, h = self._ve(eng)
        return self.f.op(e, lambda: h.memset(ap, val), (), W)

    def load(self, out, in_, sbt, R=(), q=None):
        nc = self.nc
        return self.f.dma(self.f.sp, lambda: nc.sync.dma_start(out=out, in_=in_), R, [sbt], sb=sbt)

    def store(self, out, in_, sbt, W=()):
        nc = self.nc
        return self.f.dma(self.f.pool, lambda: nc.gpsimd.dma_start(out=out, in_=in_), [sbt], list(W), sb=sbt)

    def load_cast_weight(self, st, name, src2d, kchunks, ncols, eng_cycle=("pool", "dve")):
        w = self.sb(st, name, [128, kchunks, ncols], BF16)
        i = 0
        for c in range(kchunks):
            for n0 in range(0, ncols, 1024):
                n1 = min(ncols, n0 + 1024)
                stg = self.wstage[self.wst_i % len(self.wstage)]
                self.wst_i += 1
                self.load(stg.t[:, 0:n1 - n0], src2d[c * 128:(c + 1) * 128, n0:n1], stg)
                self.cp(w.t[:, c, n0:n1], stg.t[:, 0:n1 - n0], [stg], [w], eng=eng_cycle[i % len(eng_cycle)])
                i += 1
        return w

    def bcast_row(self, st, name, row_ap):
        g = self.sb(st, name, [128, D], F32)
        self.load(g.t[:], row_ap.partition_broadcast(128), g)
        return g

    def rms_hT(self, xt, gb, hT, pbank):
        W = self.wk
        self.act(W["hb"].t[:], xt.t[:], AF.Square, [xt], [W["hb"], W["ss"]], accum_out=W["ss"].t[:, 0:1])
        self.act(W["ss"].t[:, 1:2], W["ss"].t[:, 0:1], AF.Sqrt, [W["ss"]], [W["ss"]], scale=1.0 / D, bias=1e-6)
        self.f.op(self.f.dve, lambda: self.nc.vector.reciprocal(out=W["ss"].t[:, 2:3], in_=W["ss"].t[:, 1:2]), [W["ss"]], [W["ss"]])
        self.stt(W["hb"].t[:], xt.t[:], W["ss"].t[:, 2:3], gb.t[:], ALU.mult, ALU.mult, [xt, W["ss"], gb], [W["hb"]])
        self.transpose_to(W["hb"].t, W["hb"], hT, pbank)

    def transpose_to(self, src, srct, dstT, pbank, eng="act"):
        pb = self.PS[:, pbank, :].bitcast(BF16)
        for c in range(8):
            self.tr(pb[:, c * 128:(c + 1) * 128], src[:, c * 128:(c + 1) * 128], self.C["identb"].t[:], [srct, self.C["identb"]], [self.pk[pbank]])
        self.cp(dstT.t[:].rearrange("p c t -> p (c t)"), pb, [self.pk[pbank]], [dstT], eng=eng)

    def build(self):
        cfg = self.cfg
        nc = self.nc
        NT, NTT, SEQ, NS = cfg.NT, cfg.NTT, cfg.SEQ, cfg.NS
        inp, outp = self.inp, self.outp
        xp = inp("xp", [SEQ, D])
        xs = inp("xs", [128, D])
        pool = inp("pool", [2 * cfg.NPHYS * 128, 1024])
        winst = inp("winst", [2 * NS * 512, 512])
        hst = inp("hst", [2 * NS * 8 * 128, 128])
        ptab = inp("ptab", [1, NS * cfg.NPG], I32)
        pp = inp("pp", [4 * SEQ, PLE])
        psm = inp("psm", [4 * 128, PLE])
        norm_mix = inp("norm_mix", [4, D])
        norm_mlp = inp("norm_mlp", [4, D])
        norm_ple = inp("norm_ple", [4, D])
        norm_final = inp("norm_final", [1, D])
        nsa_w_in = inp("nsa_w_in", [2 * D, NSA_IN])
        cmp_pe = inp("cmp_pe", [2 * 2 * 32, 64])
        cmp_w1 = inp("cmp_w1", [2 * 2 * 32 * 64, 256])
        cmp_w2 = inp("cmp_w2", [2 * 2 * 256, 64])
        nsa_w_out = inp("nsa_w_out", [2 * D, D])
        hg_w_in = inp("hg_w_in", [2 * D, 4 * D])
        hg_lb = inp("hg_lb", [2, D])
        hg_norm = inp("hg_norm", [2, 128])
        hg_w_out = inp("hg_w_out", [2 * D, D])
        mlp_w1 = inp("mlp_w1", [4 * D, DFF])
        mlp_w2 = inp("mlp_w2", [4 * DFF, D])
        ple_proj = inp("ple_proj", [4 * PLE, D])
        ple_gate = inp("ple_gate", [4 * D, D])
        hc = host_consts(cfg)
        for k, v in hc.items():
            inp("c_" + k, list(v.shape), BF16 if v.dtype == ml_dtypes.bfloat16 else F32)
        cdram = CD(self.din)
        y_p = outp("y_p", [SEQ, D])
        y_s = outp("y_s", [128, D])
        nkv_p = outp("nkv_p", [2 * SEQ, 1024])
        nkv_s = outp("nkv_s", [2 * 128, 1024])
        nwin_p = outp("nwin_p", [2 * 512, 512])
        nwin_s = outp("nwin_s", [2 * NS * 512, 512])
        nh_p = outp("nh_p", [2 * 8 * 128, 128])
        nh_s = outp("nh_s", [2 * NS * 8 * 128, 128])
        xres = nc.dram_tensor("xres", [NTT * 128, D], F32, kind="Internal").ap()
        self.xres = xres
        self.xres_k = [Trk("xres%d" % i) for i in range(NTT)]
        self.outk = Trk("outs")

        with ExitStack() as st0:
            self.f = FW(nc, st0)
            self.PS = st0.enter_context(nc.psum_tensor("PS", [128, 8, 512], F32))
            self.pk = [Trk("bank%d" % i) for i in range(8)]
            self.C = {}
            for k in ("identb", "identf", "I4", "BcT", "BcauT", "BwinT", "keep0", "forced0", "pairc", "iota_p",
                      "I8x4", "Bnew0", "BwinS", "pairS", "maskA", "maskS", "rowsel"):
                v = hc[k]
                t = self.sb(st0, "C_" + k, list(v.shape), BF16 if v.dtype == ml_dtypes.bfloat16 else F32)
                self.load(t.t[:], cdram[k][:, :], t)
                self.C[k] = t
            self.cdram = cdram
            self.wk = {}
            for nm, shp, dt in (("ss", [128, 4], F32), ("hb", [128, D], BF16)):
                self.wk[nm] = self.sb(st0, "wk_" + nm, shp, dt)
            self.wstage = [self.sb(st0, "wstg%d" % i, [128, 1024], F32) for i in range(2)]
            self.wst_i = 0

            for tt in range(NTT):
                src = self.din["xp"][tt * 128:(tt + 1) * 128, :] if tt < NT else self.din["xs"][:, :]
                self.f.dma(self.f.sp, lambda: nc.sync.dma_start(out=xres[tt * 128:(tt + 1) * 128, :], in_=src), (), [self.xres_k[tt]])

            for layer in range(cfg.depth):
                if layer % 2 == 0:
                    if "nsa" in self.stages:
                        self.nsa_layer(layer)
                else:
                    if "hg" in self.stages:
                        self.hg_layer(layer)
                if "post" in self.stages:
                    self.post_layer(layer, last=(layer == cfg.depth - 1))
            if "post" not in self.stages:
                for tt in range(NTT):
                    dst = y_p[tt * 128:(tt + 1) * 128, :] if tt < NT else y_s[:, :]
                    self.f.dma(self.f.sp, lambda: nc.sync.dma_start(out=dst, in_=xres[tt * 128:(tt + 1) * 128, :]), [self.xres_k[tt]], [self.outk])
            self.f.finish()
        return nc

    def post_layer(self, layer, last):
        cfg = self.cfg
        nc = self.nc
        NT, NTT = cfg.NT, cfg.NTT
        din = self.din
        with ExitStack() as st:
            gm = self.bcast_row(st, "g_mlp", din["norm_mlp"][layer, :])
            gp = self.bcast_row(st, "g_ple", din["norm_ple"][layer, :])
            gf = self.bcast_row(st, "g_fin", din["norm_final"][0, :]) if last else None
            wg = self.load_cast_weight(st, "wgate", din["ple_gate"][layer * D:(layer + 1) * D, :], 8, D)
            wp = self.load_cast_weight(st, "wproj", din["ple_proj"][layer * PLE:(layer + 1) * PLE, :], 2, D)
            GT = 4
            xg = [self.sb(st, "xg%d" % i, [128, D], F32) for i in range(GT)]
            hTg = self.sb(st, "hTg", [128, 8, GT * 128], BF16)
            uT = self.sb(st, "uT", [128, 32, GT * 128], BF16)
            w1s = [self.sb(st, "w1s%d" % i, [128, 8, 128], BF16) for i in range(2)]
            w1f = [self.sb(st, "w1f%d" % i, [128, 8, 128], F32) for i in range(2)]
            w2s = [self.sb(st, "w2s%d" % i, [128, 512], BF16) for i in range(2)]
            w2f = [self.sb(st, "w2f%d" % i, [128, 512], F32) for i in range(2)]
            hT1 = self.sb(st, "hT1", [128, 8, 128], BF16)
            pt_f = self.sb(st, "pt_f", [128, PLE], F32)
            pt_b = self.sb(st, "pt_b", [128, PLE], BF16)
            pT = self.sb(st, "pT", [128, 2, 128], BF16)
            gate = self.sb(st, "gate", [128, D], F32)
            yo = self.sb(st, "yo", [128, D], F32)
            rl = self.sb(st, "rl", [128, 512], F32)
            w1d = din["mlp_w1"]
            w2d = din["mlp_w2"]
            groups = [list(range(g0, min(g0 + GT, NTT))) for g0 in range(0, NTT, GT)]
            for grp in groups:
                ng = len(grp)
                ncol = ng * 128
                for i, tt in enumerate(grp):
                    self.load(xg[i].t[:], self.xres[tt * 128:(tt + 1) * 128, :], xg[i], R=[self.xres_k[tt]])
                    self.rms_hT(xg[i], gm, hT1, 7)
                    self.cp(hTg.t[:, :, i * 128:(i + 1) * 128], hT1.t[:], [hT1], [hTg], eng="pool")
                for j in range(32):
                    s = j % 2
                    self.load(w1f[s].t[:], w1d[layer * D:(layer + 1) * D, j * 128:(j + 1) * 128].rearrange("(c p) n -> p c n", p=128), w1f[s])
                    self.cp(w1s[s].t[:], w1f[s].t[:], [w1f[s]], [w1s[s]], eng="pool")
                    bank = 4 + (j % 2)
                    for c in range(8):
                        self.mm(self.PS[:, bank, 0:ncol], w1s[s].t[:, c, :], hTg.t[:, c, 0:ncol], c == 0, c == 7, [w1s[s], hTg], [self.pk[bank]])
                    self.act(rl.t[:, 0:ncol], self.PS[:, bank, 0:ncol], AF.Relu, [self.pk[bank]], [rl])
                    self.tt(uT.t[:, j, 0:ncol], rl.t[:, 0:ncol], rl.t[:, 0:ncol], ALU.mult, [rl], [uT])
                for half in range(2):
                    for j in range(32):
                        s = j % 2
                        self.load(w2f[s].t[:], w2d[layer * DFF + j * 128:layer * DFF + (j + 1) * 128, half * 512:(half + 1) * 512], w2f[s])
                        self.cp(w2s[s].t[:], w2f[s].t[:], [w2f[s]], [w2s[s]], eng="pool")
                        for i in range(ng):
                            self.mm(self.PS[:, i, :], uT.t[:, j, i * 128:(i + 1) * 128], w2s[s].t[:], j == 0, j == 31, [uT, w2s[s]], [self.pk[i]])
                    for i in range(ng):
                        self.tt(xg[i].t[:, half * 512:(half + 1) * 512], xg[i].t[:, half * 512:(half + 1) * 512], self.PS[:, i, :], ALU.add, [xg[i], self.pk[i]], [xg[i]])
                for i, tt in enumerate(grp):
                    self.rms_hT(xg[i], gp, hT1, 7)
                    for half in range(2):
                        bank = 4 + half
                        for c in range(8):
                            self.mm(self.PS[:, bank, :], hT1.t[:, c, :], wg.t[:, c, half * 512:(half + 1) * 512], c == 0, c == 7, [hT1, wg], [self.pk[bank]])
                        self.act(gate.t[:, half * 512:(half + 1) * 512], self.PS[:, bank, :], AF.Sigmoid, [self.pk[bank]], [gate])
                    psrc = din["pp"][layer * cfg.SEQ + tt * 128:layer * cfg.SEQ + (tt + 1) * 128, :] if tt < NT else din["psm"][layer * 128:(layer + 1) * 128, :]
                    self.load(pt_f.t[:], psrc, pt_f)
                    self.cp(pt_b.t[:], pt_f.t[:], [pt_f], [pt_b], eng="pool")
                    pb = self.PS[:, 6, :].bitcast(BF16)
                    for c in range(2):
                        self.tr(pb[:, c * 128:(c + 1) * 128], pt_b.t[:, c * 128:(c + 1) * 128], self.C["identb"].t[:], [pt_b, self.C["identb"]], [self.pk[6]])
                    self.cp(pT.t[:].rearrange("p c t -> p (c t)"), pb[:, 0:256], [self.pk[6]], [pT], eng="act")
                    for half in range(2):
                        bank = 4 + half
                        for c in range(2):
                            self.mm(self.PS[:, bank, :], pT.t[:, c, :], wp.t[:, c, half * 512:(half + 1) * 512], c == 0, c == 1, [pT, wp], [self.pk[bank]])
                        self.tt(gate.t[:, half * 512:(half + 1) * 512], gate.t[:, half * 512:(half + 1) * 512], self.PS[:, bank, :], ALU.mult, [gate, self.pk[bank]], [gate])
                    self.tt(xg[i].t[:], xg[i].t[:], gate.t[:], ALU.add, [xg[i], gate], [xg[i]], eng="pool")
                    if not last:
                        self.store(self.xres[tt * 128:(tt + 1) * 128, :], xg[i].t[:], xg[i], W=[self.xres_k[tt]])
                    else:
                        W = self.wk
                        self.act(W["hb"].t[:], xg[i].t[:], AF.Square, [xg[i]], [W["hb"], W["ss"]], accum_out=W["ss"].t[:, 0:1])
                        self.act(W["ss"].t[:, 1:2], W["ss"].t[:, 0:1], AF.Sqrt, [W["ss"]], [W["ss"]], scale=1.0 / D, bias=1e-6)
                        self.f.op(self.f.dve, lambda: nc.vector.reciprocal(out=W["ss"].t[:, 2:3], in_=W["ss"].t[:, 1:2]), [W["ss"]], [W["ss"]])
                        self.stt(yo.t[:], xg[i].t[:], W["ss"].t[:, 2:3], gf.t[:], ALU.mult, ALU.mult, [xg[i], W["ss"], gf], [yo])
                        dst = self.dout["y_p"][tt * 128:(tt + 1) * 128, :] if tt < NT else self.dout["y_s"][:, :]
                        self.store(dst, yo.t[:], yo, W=[self.outk])
            self.f.barrier()
            self.f.release([gm, gp, wg, wp, hTg, uT, hT1, pt_f, pt_b, pT, gate, yo] + xg + w1s + w1f + w2s + w2f + ([gf] if gf else []))

    def nsa_layer(self, layer):
        a = layer // 2
        cfg = self.cfg
        nc = self.nc
        NT, SEQ, NS = cfg.NT, cfg.SEQ, cfg.NS
        C, PS, pk, din, dout = self.C, self.PS, self.pk, self.din, self.dout
        idb = C["identb"]
        with ExitStack() as st:
            wout = self.load_cast_weight(st, "nwout", din["nsa_w_out"][a * D:(a + 1) * D, :], 8, D)
            w1c = self.sb(st, "w1c", [64, 2, 32, 256], BF16)
            for kv in range(2):
                for l0 in range(0, 32, 4):
                    stg = self.wstage[self.wst_i % 2]
                    self.wst_i += 1
                    r0 = ((a * 2 + kv) * 32 + l0) * 64
                    sv = stg.t[0:64, :].rearrange("p (l h) -> p l h", l=4)
                    self.load(sv, din["cmp_w1"][r0:r0 + 256, :].rearrange("(l d) h -> d l h", d=64), stg)
                    self.cp(w1c.t[:, kv, l0:l0 + 4, :], sv, [stg], [w1c], eng="pool")
            w2c = self.sb(st, "w2c", [128, 2, 2, 64], BF16)
            stg = self.wstage[self.wst_i % 2]
            self.wst_i += 1
            sv = stg.t[:, 0:256].rearrange("p (kv ch d) -> p kv ch d", kv=2, ch=2)
            self.load(sv, din["cmp_w2"][a * 512:(a + 1) * 512, :].rearrange("(kv ch p) d -> p kv ch d", kv=2, ch=2), stg)
            self.cp(w2c.t[:], sv, [stg], [w2c], eng="pool")
            pef = self.sb(st, "pef", [64, 64], F32)
            peT = self.sb(st, "peT", [64, 66], BF16)
            b1c = self.sb(st, "b1c", [128, 4], F32)
            self.load(pef.t[:], din["cmp_pe"][a * 64:(a + 1) * 64, :], pef)
            self.tr(PS[0:64, 7, 0:64], pef.t[:], C["identf"].t[0:64, 0:64], [pef, C["identf"]], [pk[7]])
            self.mset(peT.t[:], 0.0, [peT])
            self.cp(peT.t[:, 0:64], PS[0:64, 7, 0:64], [pk[7]], [peT])
            for kv in range(2):
                for ch in range(2):
                    idx = kv * 2 + ch
                    for l in range(32):
                        self.mm(PS[:, 6, idx * 2:idx * 2 + 2], w1c.t[:, kv, l, ch * 128:(ch + 1) * 128], peT.t[:, kv * 32 + l:kv * 32 + l + 2], l == 0, l == 31, [w1c, peT], [pk[6]])
            self.cp(b1c.t[:], PS[:, 6, 0:8].rearrange("p (i two) -> p i two", two=2)[:, :, 0], [pk[6]], [b1c])
            xt = self.sb(st, "xt", [128, D], F32)
            kvb = self.sb(st, "kvb", [128, 1536], BF16)
            qTm = [self.sb(st, "qTm%d" % i, [128, 8, 128], BF16) for i in range(2)]
            self.mset(qTm[0].t[:], 0.0, [qTm[0]])
            self.mset(qTm[1].t[:], 0.0, [qTm[1]])
            gts = self.sb(st, "gts", [128, 48], F32)
            E = [self.sb(st, "E%d" % i, [128, 512], BF16) for i in range(2)]
            att = self.sb(st, "att", [128, 16, 64], F32)
            attb = self.wk["hb"]
            attT = self.sb(st, "attT", [128, 8, 128], BF16)
            KsTn = self.sb(st, "KsTn", [128, 2, 128], BF16)
            KwTn = self.sb(st, "KwTn", [128, 2, 128], BF16)
            Vsn = self.sb(st, "Vsn", [128, 4, 65], BF16)
            Vwn = self.sb(st, "Vwn", [128, 4, 65], BF16)
            stb = ExitStack()
            gmx = self.bcast_row(stb, "g_mix", din["norm_mix"][layer, :])
            win = self.load_cast_weight(stb, "nwin", din["nsa_w_in"][a * D:(a + 1) * D, :], 8, NSA_IN)
            KsT = self.sb(stb, "KsT", [128, 2, SEQ], BF16)
            KwT = self.sb(stb, "KwT", [128, 2, 8 * 128], BF16)
            Vs = self.sb(stb, "Vs", [128, NT, 4, 65], BF16)
            Vw = self.sb(stb, "Vw", [128, 8, 4, 65], BF16)
            kcT = self.sb(stb, "kcT", [128, 2, 128], BF16)
            vcS = self.sb(stb, "vcS", [64, 4, 128], BF16)
            VCX = self.sb(stb, "VCX", [128, 4, 129], BF16)
            self.mset(kcT.t[:], 0.0, [kcT])
            self.mset(vcS.t[:], 0.0, [vcS])
            self.mset(Vs.t[:, :, :, 64:65], 1.0, [Vs])
            self.mset(Vw.t[:, :, :, 64:65], 1.0, [Vw])
            self.mset(VCX.t[:, :, 64:65], 1.0, [VCX])
            for g in range(4):
                self.cp(VCX.t[:, g, 65:129], C["pairc"].t[:], [C["pairc"]], [VCX], eng="pool")
            cs = self.sb(stb, "cs", [128, 64], F32)
            hT = attT
            pf = self.sb(stb, "pf", [128, NSA_IN], F32)
            kvr = self.sb(stb, "kvr", [128, 1536], F32)
            qr = self.sb(stb, "qr", [128, 16, 64], BF16)
            kpair = self.sb(stb, "kpair", [128, 2, 4, 64], BF16)
            t1 = self.sb(stb, "t1", [128, 16, 32], F32)
            t2 = self.sb(stb, "t2", [128, 16, 32], F32)
            kcm = self.sb(stb, "kcm", [64, 2, 4, 128], BF16)
            xg_ = self.sb(stb, "xgel", [128, 64], F32)
            ug_ = self.sb(stb, "ugel", [128, 64], F32)
            Hg = self.sb(stb, "Hg", [128, 64], BF16)
            den4 = self.sb(stb, "den4", [128, 4], F32)
            rec4 = self.sb(stb, "rec4", [128, 4], F32)
            coef4 = self.sb(stb, "coef4", [128, 4], F32)
            imp = self.sb(stb, "imp", [128, 64], F32)
            imp2 = self.sb(stb, "imp2", [128, 64], F32)
            m8 = self.sb(stb, "m8", [128, 16], F32)
            selb = self.sb(stb, "selb", [128, 64], F32)
            selx2 = [self.sb(stb, "selx2_%d" % i, [128, 2, 64], BF16) for i in range(2)]
            sxc = [0]
            ecnt = [0]
            scnt = [0]

            def rope(src4, dst4, nh_shape, cosb, sinb, Rs, Wd, tmpa, tmpb):
                s1, s2 = src4
                d1, d2 = dst4
                self.tt(tmpa, s1, cosb, ALU.mult, Rs + [cs], [t1])
                self.tt(tmpb, s2, sinb, ALU.mult, Rs + [cs], [t2], eng="pool")
                self.tt(d1, tmpa, tmpb, ALU.subtract, [t1, t2], Wd)
                self.tt(tmpa, s1, sinb, ALU.mult, Rs + [cs], [t1], eng="pool")
                self.tt(tmpb, s2, cosb, ALU.mult, Rs + [cs], [t2])
                self.tt(d2, tmpa, tmpb, ALU.add, [t1, t2], Wd, eng="pool")

            def stage_a(tt):
                is_s = tt == NT
                self.load(xt.t[:], self.xres[tt * 128:(tt + 1) * 128, :], xt, R=[self.xres_k[tt]])
                self.load(cs.t[:, 0:32], self.cdram["ropec"][tt * 128:(tt + 1) * 128, :], cs)
                self.load(cs.t[:, 32:64], self.cdram["ropes"][tt * 128:(tt + 1) * 128, :], cs)
                self.rms_hT(xt, gmx, hT, 7)
                for nb in range(6):
                    n0 = nb * 512
                    n1 = min(NSA_IN, n0 + 512)
                    for c in range(8):
                        self.mm(PS[:, nb, 0:n1 - n0], hT.t[:, c, :], win.t[:, c, n0:n1], c == 0, c == 7, [hT, win], [pk[nb]])
                for nb in range(6):
                    n0 = nb * 512
                    n1 = min(NSA_IN, n0 + 512)
                    self.cp(pf.t[:, n0:n1], PS[:, nb, 0:n1 - n0], [pk[nb]], [pf], eng=("act" if nb % 2 == 0 else "dve"))
                self.act(gts.t[:], pf.t[:, 2560:2608], AF.Sigmoid, [pf], [gts])
                qv = pf.t[:, 0:1024].rearrange("p (h d) -> p h d", h=16)
                cosq = cs.t[:, 0:32].unsqueeze(1).broadcast_to([128, 16, 32])
                sinq = cs.t[:, 32:64].unsqueeze(1).broadcast_to([128, 16, 32])
                qv4 = pf.t[:, 0:1024].rearrange("p (half hl d) -> p half hl d", half=2, hl=8)
                qd4 = qr.t[:].rearrange("p (hl half) d -> p half hl d", half=2)
                cosq4 = cs.t[:, 0:32].unsqueeze(1).unsqueeze(1).broadcast_to([128, 2, 8, 32])
                sinq4 = cs.t[:, 32:64].unsqueeze(1).unsqueeze(1).broadcast_to([128, 2, 8, 32])
                t14 = t1.t[:].rearrange("p (half hl) d -> p half hl d", half=2)
                t24 = t2.t[:].rearrange("p (half hl) d -> p half hl d", half=2)
                rope((qv4[:, :, :, 0:32], qv4[:, :, :, 32:64]), (qd4[:, :, :, 0:32], qd4[:, :, :, 32:64]), None, cosq4, sinq4, [pf], [qr], t14, t24)
                kvs = pf.t[:, 1024:2560].rearrange("p (ty kv g d) -> p ty kv g d", ty=3, kv=2, g=4)
                kvd = kvr.t[:].rearrange("p (ty kv g d) -> p ty kv g d", ty=3, kv=2, g=4)
                cosk = cs.t[:, 0:32].unsqueeze(1).unsqueeze(1).broadcast_to([128, 3, 4, 32])
                sink = cs.t[:, 32:64].unsqueeze(1).unsqueeze(1).broadcast_to([128, 3, 4, 32])
                ta = t1.t[:, 0:12, :].rearrange("p (ty g) d -> p ty g d", ty=3)
                tb = t2.t[:, 0:12, :].rearrange("p (ty g) d -> p ty g d", ty=3)
                rope((kvs[:, :, 0, :, 0:32], kvs[:, :, 0, :, 32:64]), (kvd[:, :, 0, :, 0:32], kvd[:, :, 0, :, 32:64]), None, cosk, sink, [pf], [kvr], ta, tb)
                self.cp(kvd[:, :, 1], kvs[:, :, 1], [pf], [kvr], eng="pool")
                if not is_s:
                    self.store(dout["nkv_p"][a * SEQ + tt * 128:a * SEQ + (tt + 1) * 128, :], kvr.t[:, 0:1024], kvr, W=[self.outk])
                    if tt * 128 >= SEQ - 512:
                        o0 = a * 512 + tt * 128 - (SEQ - 512)
                        self.store(dout["nwin_p"][o0:o0 + 128, :], kvr.t[:, 1024:1536], kvr, W=[self.outk])
                else:
                    self.store(dout["nkv_s"][a * 128:(a + 1) * 128, :], kvr.t[:, 0:1024], kvr, W=[self.outk])
                    for j in range(NS):
                        r0 = (a * NS + j) * 512 + 504
                        self.store(dout["nwin_s"][r0:r0 + 8, :], kvr.t[8 * j:8 * j + 8, 1024:1536], kvr, W=[self.outk])
                    srcw = din["winst"][a * NS * 512:(a + 1) * NS * 512, :].rearrange("(j u) c -> j (u c)", u=512)[:, 8 * 512:]
                    dstw = dout["nwin_s"][a * NS * 512:(a + 1) * NS * 512, :].rearrange("(j u) c -> j (u c)", u=512)[:, 0:504 * 512]
                    self.f.dma(self.f.pool, lambda: nc.gpsimd.dma_start(out=dstw, in_=srcw), (), [self.outk])
                self.cp(kvb.t[:], kvr.t[:], [kvr], [kvb], eng="pool")
                pb0 = PS[:, 0, :].bitcast(BF16)
                for hl in range(8):
                    self.tr(pb0[:, hl * 128:(hl + 1) * 128], qr.t[:, 2 * hl:2 * hl + 2, :].rearrange("p a d -> p (a d)"), idb.t[:], [qr, idb], [pk[0]])
                self.cp(qTm[0].t[0:64].rearrange("p c t -> p (c t)"), pb0[0:64, :], [pk[0]], [qTm[0]], eng="act")
                self.cp(qTm[1].t[64:128].rearrange("p c t -> p (c t)"), pb0[64:128, :], [pk[0]], [qTm[1]], eng="dve")
                kb = kvb.t[:].rearrange("p (ty kv g d) -> p ty kv g d", ty=3, kv=2, g=4)
                pb1 = PS[:, 1, :].bitcast(BF16)
                pb2 = PS[:, 2, :].bitcast(BF16)
                kb6 = kvb.t[:].rearrange("p (ty kv half gl d) -> p ty kv half gl d", ty=3, kv=2, half=2, gl=2)
                kp5 = kpair.t[:].rearrange("p ty (gl half) d -> p ty gl half d", half=2)
                for half in range(2):
                    self.cp(kp5[:, :, :, half, :], kb6[:, 1:3, 0, half, :, :], [kvb], [kpair], eng="pool")
                for gl in range(2):
                    self.tr(pb1[:, gl * 128:(gl + 1) * 128], kpair.t[:, 0, 2 * gl:2 * gl + 2, :].rearrange("p a d -> p (a d)"), idb.t[:], [kpair, idb], [pk[1]])
                    self.tr(pb1[:, 256 + gl * 128:256 + (gl + 1) * 128], kpair.t[:, 1, 2 * gl:2 * gl + 2, :].rearrange("p a d -> p (a d)"), idb.t[:], [kpair, idb], [pk[1]])
                return kb, pb1, pb2

            def compress_tile(tt, kb, pb2):
                for kv in range(2):
                    for g in range(4):
                        self.tr(pb2[0:64, (kv * 4 + g) * 128:(kv * 4 + g + 1) * 128], kb[:, 0, kv, g, :], idb.t[:], [kvb, idb], [pk[2]])
                self.cp(kcm.t[:].rearrange("p kv g t -> p (kv g t)"), pb2[0:64, :], [pk[2]], [kcm], eng="act")
                import os
                CST = int(os.environ.get("NSA_CSTOP", "9"))
                if CST < 1:
                    return
                for kv in range(2):
                    for ch in range(2):
                        idx = kv * 2 + ch
                        for l in range(32):
                            self.mm(PS[:, 3, idx * 16:(idx + 1) * 16], w1c.t[:, kv, l, ch * 128:(ch + 1) * 128],
                                    kcm.t[:, kv, :, l::32], l == 0, l == 31, [w1c, kcm], [pk[3]])
                for idx in range(4):
                    self.act(xg_.t[:, idx * 16:(idx + 1) * 16], PS[:, 3, idx * 16:(idx + 1) * 16], AF.Identity, [pk[3], b1c], [xg_], bias=b1c.t[:, idx:idx + 1])
                gelu_to(xg_.t[:], ug_.t[:], Hg.t[:], xg_, ug_, Hg)
                if CST < 2:
                    return
                for half in range(2):
                    for ch in range(2):
                        self.mm(PS[half * 64:(half + 1) * 64, 4, 0:8], w2c.t[:, 0, ch, :],
                                Hg.t[:, ch * 16 + half * 8:ch * 16 + half * 8 + 8], ch == 0, ch == 1, [w2c, Hg], [pk[4]])
                for ch in range(2):
                    self.mm(PS[0:64, 4, 16:32], w2c.t[:, 1, ch, :], Hg.t[:, (2 + ch) * 16:(3 + ch) * 16], ch == 0, ch == 1, [w2c, Hg], [pk[4]])
                self.cp(kcT.t[:, :, 4 * tt:4 * tt + 4], PS[:, 4, 0:8].rearrange("p (g n) -> p g n", g=2), [pk[4]], [kcT])
                self.cp(vcS.t[:, :, 4 * tt:4 * tt + 4], PS[0:64, 4, 16:32].rearrange("p (g n) -> p g n", g=4), [pk[4]], [vcS])
                if CST < 3:
                    return
                pb5 = PS[:, 5, :].bitcast(BF16)
                for g in range(4):
                    self.tr(pb5[:, g * 64:(g + 1) * 64], vcS.t[:, g, :], idb.t[0:64, 0:64], [vcS, idb], [pk[5]])
                self.cp(VCX.t[:, :, 0:64], pb5[:, 0:256].rearrange("p (g d) -> p g d", g=4), [pk[5]], [VCX])

            def gelu_to(x, u, out, xt_, ut_, ot_):
                self.tt(u, x, x, ALU.mult, [xt_], [ut_])
                self.ts(u, u, 0.044715, 1.0, ALU.mult, ALU.add, [ut_], [ut_])
                self.tt(u, u, x, ALU.mult, [ut_, xt_], [ut_])
                self.act(u, u, AF.Sigmoid, [ut_], [ut_], scale=1.5957691216057308)
                self.tt(out, x, u, ALU.mult, [xt_, ut_], [ot_])

            self.gelu_to = gelu_to

            def evac_branch(g, br, banks, stride, first):
                for r in range(4):
                    b, c0 = banks[r]
                    self.ts(den4.t[:, r:r + 1], PS[:, b, c0 + 64:c0 + 65], 1e-30, None, ALU.max, None, [pk[b]], [den4])
                self.f.op(self.f.dve, lambda: nc.vector.reciprocal(out=rec4.t[:], in_=den4.t[:]), [den4], [rec4])
                gv = gts.t[:].rearrange("p (h b) -> p h b", b=3)[:, 4 * g:4 * g + 4, br]
                self.tt(coef4.t[:], rec4.t[:], gv, ALU.mult, [rec4, gts], [coef4])
                for r in range(4):
                    b, c0 = banks[r]
                    h = 4 * g + r
                    if first:
                        self.ts(att.t[:, h, :], PS[:, b, c0:c0 + 64], coef4.t[:, r:r + 1], None, ALU.mult, None, [pk[b], coef4], [att])
                    else:
                        self.stt(att.t[:, h, :], PS[:, b, c0:c0 + 64], coef4.t[:, r:r + 1], att.t[:, h, :], ALU.mult, ALU.add, [pk[b], coef4, att], [att])

            self_ = self

            oTs = self.sb(stb, "oTs", [65, 512], F32)

            def untranspose(ob):
                self.cp(oTs.t[:], PS[0:65, ob, :], [pk[ob]], [oTs], eng="act")
                for r in range(4):
                    self.tr(PS[:, ob, r * 65:(r + 1) * 65], oTs.t[:, r * 128:(r + 1) * 128], C["identf"].t[0:65, 0:65], [oTs, C["identf"]], [pk[ob]])

            def attn_prompt(i):
                for g in range(4):
                    half, gl = g // 2, g % 2
                    qT = qTm[half]
                    qrhs = qT.t[:, 4 * gl:4 * gl + 4, :]
                    need_sel = i >= 8
                    sbk = scnt[0] % 2
                    scnt[0] += 1
                    self.mm(PS[:, sbk, :], kcT.t[:, gl, :], qrhs, True, False, [kcT, qT], [pk[sbk]])
                    self.mm(PS[:, sbk, :], C["BcT"].t[:, 128 - 4 * i:256 - 4 * i], C["I4"].t[:], False, True, [C["BcT"], C["I4"]], [pk[sbk]])
                    e = E[ecnt[0] % 2]
                    ecnt[0] += 1
                    self.act(e.t[:], PS[:, sbk, :], AF.Exp, [pk[sbk]], [e], scale=SCALE)
                    cb = [(2, 0), (2, 129), (3, 0), (3, 129)]
                    for r in range(4):
                        b, c0 = cb[r]
                        self.mm(PS[:, b, c0:c0 + 129], e.t[:, r * 128:(r + 1) * 128], VCX.t[:, g, :], r % 2 == 0, r % 2 == 1, [e, VCX], [pk[b]], skip=True)
                    evac_branch(g, 0, cb, 129, True)
                    if need_sel:
                        for r in range(4):
                            b, c0 = cb[r]
                            if r == 0:
                                self.ts(imp.t[:], PS[:, b, c0 + 65:c0 + 129], rec4.t[:, 0:1], None, ALU.mult, None, [pk[b], rec4], [imp])
                            else:
                                self.stt(imp.t[:], PS[:, b, c0 + 65:c0 + 129], rec4.t[:, r:r + 1], imp.t[:], ALU.mult, ALU.add, [pk[b], rec4, imp], [imp])
                        self.tt(imp2.t[:], imp.t[:], C["keep0"].t[:, 64 - 2 * i:128 - 2 * i], ALU.mult, [imp, C["keep0"]], [imp2])
                        self.tt(imp2.t[:], imp2.t[:], C["forced0"].t[:, 64 - 2 * i:128 - 2 * i], ALU.add, [imp2, C["forced0"]], [imp2])
                        self.ts(imp2.t[:, 0:1], imp2.t[:, 0:1], 2e4, None, ALU.max, None, [imp2], [imp2])
                        self.f.op(self.f.dve, lambda: nc.vector.max(out=m8.t[:, 0:8], in_=imp2.t[:]), [imp2], [m8])
                        self.f.op(self.f.dve, lambda: nc.vector.match_replace(out=imp.t[:], in_to_replace=m8.t[:, 0:8], in_values=imp2.t[:], imm_value=-1e9), [imp2, m8], [imp])
                        self.f.op(self.f.dve, lambda: nc.vector.max(out=m8.t[:, 8:16], in_=imp.t[:]), [imp], [m8])
                        self.ts(selb.t[:], imp2.t[:], m8.t[:, 15:16], -NEG, ALU.is_ge, ALU.mult, [imp2, m8], [selb])
                        self.ts(selb.t[:], selb.t[:], NEG, None, ALU.add, None, [selb], [selb])
                    ob = 4
                    for c in range(i + 1):
                        sbk = scnt[0] % 2
                        scnt[0] += 1
                        last_mask = (not need_sel) and (c != i)
                        self.mm(PS[:, sbk, :], KsT.t[:, gl, c * 128:(c + 1) * 128], qrhs, True, last_mask, [KsT, qT], [pk[sbk]])
                        if need_sel:
                            sx = selx2[sxc[0] % 2]
                            sxc[0] += 1
                            self.cp(sx.t[:], selb.t[:, 2 * c:2 * c + 2].unsqueeze(2).broadcast_to([128, 2, 64]), [selb], [sx], eng="pool")
                            self.mm(PS[:, sbk, :], sx.t[:].rearrange("p a b -> p (a b)"), C["I4"].t[:], False, c != i, [sx, C["I4"]], [pk[sbk]])
                        if c == i:
                            self.mm(PS[:, sbk, :], C["BcauT"].t[:], C["I4"].t[:], False, True, [C["BcauT"], C["I4"]], [pk[sbk]])
                        e = E[ecnt[0] % 2]
                        ecnt[0] += 1
                        self.act(e.t[:], PS[:, sbk, :], AF.Exp, [pk[sbk]], [e], scale=SCALE)
                        self.mm(PS[0:65, ob, :], Vs.t[:, c, g, :], e.t[:], c == 0, c == i, [Vs, e], [pk[ob]])
                    untranspose(ob)
                    evac_branch(g, 1, [(ob, r * 65) for r in range(4)], 65, False)
                    ob = 5
                    c_lo = max(0, i - 4)
                    for c in range(c_lo, i + 1):
                        sbk = scnt[0] % 2
                        scnt[0] += 1
                        slot = c % 8
                        hasb = (c == i) or (c == i - 4)
                        self.mm(PS[:, sbk, :], KwT.t[:, gl, slot * 128:(slot + 1) * 128], qrhs, True, not hasb, [KwT, qT], [pk[sbk]])
                        if c == i:
                            self.mm(PS[:, sbk, :], C["BcauT"].t[:], C["I4"].t[:], False, True, [C["BcauT"], C["I4"]], [pk[sbk]])
                        elif c == i - 4:
                            self.mm(PS[:, sbk, :], C["BwinT"].t[:], C["I4"].t[:], False, True, [C["BwinT"], C["I4"]], [pk[sbk]])
                        e = E[ecnt[0] % 2]
                        ecnt[0] += 1
                        self.act(e.t[:], PS[:, sbk, :], AF.Exp, [pk[sbk]], [e], scale=SCALE)
                        self.mm(PS[0:65, ob, :], Vw.t[:, slot, g, :], e.t[:], c == c_lo, c == i, [Vw, e], [pk[ob]])
                    untranspose(ob)
                    evac_branch(g, 2, [(ob, r * 65) for r in range(4)], 65, False)

            def out_proj(tt, have_T=False):
                if not have_T:
                    self.cp(attb.t[:], att.t[:].rearrange("p h d -> p (h d)"), [att], [attb], eng="pool")
                    self.transpose_to(attb.t, attb, attT, 6)
                for half in range(2):
                    b = 6 + half
                    for c in range(8):
                        self.mm(PS[:, b, :], attT.t[:, c, :], wout.t[:, c, half * 512:(half + 1) * 512], c == 0, c == 7, [attT, wout], [pk[b]])
                    self.tt(xt.t[:, half * 512:(half + 1) * 512], xt.t[:, half * 512:(half + 1) * 512], PS[:, b, :], ALU.add, [xt, pk[b]], [xt])
                self.store(self.xres[tt * 128:(tt + 1) * 128, :], xt.t[:], xt, W=[self.xres_k[tt]])

            import os
            STOP = int(os.environ.get("NSA_STOP", "9"))
            for tt in range(NT if STOP >= 1 else 0):
                kb, pb1, pb2 = stage_a(tt)
                slot = tt % 8
                self.cp(KsT.t[:, :, tt * 128:(tt + 1) * 128], pb1[:, 0:256].rearrange("p (g t) -> p g t", g=2), [pk[1]], [KsT])
                self.cp(KwT.t[:, :, slot * 128:(slot + 1) * 128], pb1[:, 256:512].rearrange("p (g t) -> p g t", g=2), [pk[1]], [KwT])
                self.cp(Vs.t[:, tt, :, 0:64], kb[:, 1, 1], [kvb], [Vs], eng="pool")
                self.cp(Vw.t[:, slot, :, 0:64], kb[:, 2, 1], [kvb], [Vw], eng="pool")
                if STOP >= 2:
                    compress_tile(tt, kb, pb2)
                if STOP >= 3:
                    attn_prompt(tt)
                if STOP >= 4:
                    out_proj(tt)
            kb, pb1, pb2 = stage_a(NT)
            self.cp(KsTn.t[:], pb1[:, 0:256].rearrange("p (g t) -> p g t", g=2), [pk[1]], [KsTn])
            self.cp(KwTn.t[:], pb1[:, 256:512].rearrange("p (g t) -> p g t", g=2), [pk[1]], [KwTn])
            self.mset(Vsn.t[:, :, 64:65], 1.0, [Vsn])
            self.mset(Vwn.t[:, :, 64:65], 1.0, [Vwn])
            self.cp(Vsn.t[:, :, 0:64], kb[:, 1, 1], [kvb], [Vsn], eng="pool")
            self.cp(Vwn.t[:, :, 0:64], kb[:, 2, 1], [kvb], [Vwn], eng="pool")
            self.f.barrier()
            self.f.release([gmx, cs, kvr])
            stb.close()
            self.nsa_sample(a, dict(locals()))
            out_proj(NT, have_T=True)
            self.f.barrier()
            self.f.release([wout, w1c, w2c, pef, xt])

    def nsa_sample(self, a, L):
        cfg = self.cfg
        nc = self.nc
        NS, NPG = cfg.NS, cfg.NPG
        C, PS, pk, din, dout = self.C, self.PS, self.pk, self.din, self.dout
        idb = C["identb"]
        qTm, kvb, gts, att, attT, w1c, w2c, b1c = L["qTm"], L["kvb"], L["gts"], L["att"], L["attT"], L["w1c"], L["w2c"], L["b1c"]
        KsTn, KwTn, Vsn, Vwn = L["KsTn"], L["KwTn"], L["Vsn"], L["Vwn"]
        NK = NPG * 128
        with ExitStack() as st:
            sb = lambda n, shp, dt: self.sb(st, n, shp, dt)
            pti = sb("pti", [128, NS * NPG], I32)
            idxf = sb("idxf", [128, NS * NPG], F32)
            idx = sb("idx", [128, NS * NPG], I32)
            self.load(pti.t[:], din["ptab"][0, :].partition_broadcast(128), pti)
            self.ts(idxf.t[:], pti.t[:], 128.0, float(a * cfg.NPHYS * 128), ALU.mult, ALU.add, [pti], [idxf])
            self.tt(idx.t[:], idxf.t[:], C["iota_p"].t[:, 0:1].broadcast_to([128, NS * NPG]), ALU.add, [idxf, C["iota_p"]], [idx])
            pgf = sb("pgf", [128, 1024], F32)
            pgb = sb("pgb", [128, 1024], BF16)
            kp = sb("kp", [128, 4, 64], BF16)
            KsTj = sb("KsTj", [128, 2, NK], BF16)
            kcmj = sb("kcmj", [64, 2, 4, 1024], BF16)
            Vsj = sb("Vsj", [128, NPG, 4, 65], BF16)
            self.mset(Vsj.t[:, :, :, 64:65], 1.0, [Vsj])
            xg = sb("sxg", [128, 4, 128], F32)
            ug = sb("sug", [128, 4, 128], F32)
            Hgj = sb("Hgj", [128, 4, 128], BF16)
            kcTj = sb("kcTj", [128, 2, 64], BF16)
            vcSj = sb("vcSj", [64, 4, 64], BF16)
            VCXj = sb("VCXj", [64, 4, 97], BF16)
            self.mset(VCXj.t[:, :, 64:65], 1.0, [VCXj])
            for g in range(4):
                self.cp(VCXj.t[:, g, 65:97], C["pairS"].t[:], [C["pairS"]], [VCXj], eng="pool")
            Es = [sb("Es%d" % i, [128, 128], BF16) for i in range(2)]
            oc = sb("oc", [8, 16, 97], F32)
            den16 = sb("den16", [8, 16], F32)
            rec16 = sb("rec16", [8, 16], F32)
            coef16 = sb("coef16", [8, 16], F32)
            gj = sb("gj", [8, 48], F32)
            tmpi = sb("tmpi", [8, 16, 32], F32)
            impj = sb("impj", [8, 4, 32], F32)
            imp33 = sb("imp33", [8, 4, 40], F32)
            wk33 = sb("wk33", [8, 40], F32)
            m8 = sb("sm8", [8, 16], F32)
            selbj = sb("selbj", [8, 4, 32], F32)
            selx = sb("selx", [8, 4, 32, 64], BF16)
            attj = sb("attj", [8, 16, 64], F32)
            atmp = sb("atmp", [8, 16, 64], F32)
            attjb = sb("attjb", [8, 1024], BF16)
            wbf = sb("wbf", [128, 4, 512], F32)
            wbb = sb("wbb", [128, 4, 512], BF16)
            KwTj = sb("KwTj", [128, 2, 512], BF16)
            Vwj = sb("Vwj", [128, 4, 4, 65], BF16)
            self.mset(Vwj.t[:, :, :, 64:65], 1.0, [Vwj])
            self.mset(imp33.t[:], -2.0, [imp33])
            self.mset(imp33.t[:, :, 0:1], 2e4, [imp33])
            self.mset(imp33.t[:, :, 32:33], 3e4, [imp33])
            gview = gts
            scnt = [0]
            ecnt = [0]
            pool_ap = din["pool"]

            def obank(h):
                return 2 + h // 4, (h % 4) * 128

            def kpair_T(src_k4, srct, dstT_ap, dstt, bank):
                s5 = src_k4.rearrange("p (half gl) d -> p half gl d", half=2)
                k5 = kp.t[:].rearrange("p (gl half) d -> p gl half d", half=2)
                for half in range(2):
                    self.cp(k5[:, :, half, :], s5[:, half, :, :], [srct], [kp], eng="pool")
                pb = PS[:, bank, :].bitcast(BF16)
                for gl in range(2):
                    self.tr(pb[:, gl * 128:(gl + 1) * 128], kp.t[:, 2 * gl:2 * gl + 2, :].rearrange("p a d -> p (a d)"), idb.t[:], [kp, idb], [pk[bank]])
                self.cp(dstT_ap, pb[:, 0:256].rearrange("p (g t) -> p g t", g=2), [pk[bank]], [dstt])

            def evac(br, width, first):
                for b in range(4):
                    self.cp(oc.t[:, 4 * b:4 * b + 4, 0:width], PS[0:8, 2 + b, :].rearrange("p (h c) -> p h c", h=4)[:, :, 0:width], [pk[2 + b]], [oc], eng=("act" if b % 2 == 0 else "dve"))
                self.ts(den16.t[:], oc.t[:, :, 64], 1e-30, None, ALU.max, None, [oc], [den16])
                self.f.op(self.f.dve, lambda: nc.vector.reciprocal(out=rec16.t[:], in_=den16.t[:]), [den16], [rec16])
                self.tt(coef16.t[:], rec16.t[:], gj.t[:].rearrange("p (h b) -> p h b", b=3)[:, :, br], ALU.mult, [rec16, gj], [coef16])
                cb = coef16.t[:].unsqueeze(2).broadcast_to([8, 16, 64])
                if first:
                    self.tt(attj.t[:], oc.t[:, :, 0:64], cb, ALU.mult, [oc, coef16], [attj])
                else:
                    self.tt(atmp.t[:], oc.t[:, :, 0:64], cb, ALU.mult, [oc, coef16], [atmp])
                    self.tt(attj.t[:], attj.t[:], atmp.t[:], ALU.add, [attj, atmp], [attj], eng="pool")

            def score_chunk(lhs_fn, lhst, j, M, bias_lhsT=None, biast=None, per_g_bias=None):
                sbk = scnt[0] % 2
                scnt[0] += 1
                for g in range(4):
                    half, gl = g // 2, g % 2
                    rhs = qTm[half].t[:, 4 * gl:4 * gl + 4, 8 * j:8 * j + 8]
                    hasb = (bias_lhsT is not None) or (per_g_bias is not None)
                    self.mm(PS[0:M, sbk, g * 32:(g + 1) * 32], lhs_fn(g, gl), rhs, True, not hasb, [lhst, qTm[half]], [pk[sbk]])
                    if per_g_bias is not None:
                        self.mm(PS[0:M, sbk, g * 32:(g + 1) * 32], per_g_bias(g), C["I8x4"].t[:], False, True, [selx, C["I8x4"]], [pk[sbk]])
                    elif bias_lhsT is not None:
                        self.mm(PS[0:M, sbk, g * 32:(g + 1) * 32], bias_lhsT, C["I8x4"].t[:], False, True, [biast, C["I8x4"]], [pk[sbk]])
                e = Es[ecnt[0] % 2]
                ecnt[0] += 1
                self.act(e.t[0:M, :], PS[0:M, sbk, 0:128], AF.Exp, [pk[sbk]], [e], scale=SCALE)
                return e

            oTj = sb("oTj", [65, 128], F32)

            def pv(e, M, rhs_fn, rhst, width, first, last):
                if width == 65:
                    for g in range(4):
                        self.mm(PS[0:65, 2, g * 32:(g + 1) * 32], rhs_fn(g), e.t[0:M, g * 32:(g + 1) * 32], first and g == 0, last and g == 3, [e, rhst], [pk[2]], skip=True)
                    if last:
                        self.cp(oTj.t[:], PS[0:65, 2, 0:128], [pk[2]], [oTj], eng="act")
                        for h in range(16):
                            b, c0 = obank(h)
                            self.tr(PS[0:8, b, c0:c0 + 65], oTj.t[:, h * 8:(h + 1) * 8], C["identf"].t[0:65, 0:65], [oTj, C["identf"]], [pk[b]])
                    return
                for h in range(16):
                    b, c0 = obank(h)
                    self.mm(PS[0:8, b, c0:c0 + width], e.t[0:M, h * 8:(h + 1) * 8], rhs_fn(h // 4), first and (h % 4 == 0), last and (h % 4 == 3), [e, rhst], [pk[b]], skip=True)

            for j in range(NS):
                self.mm(PS[0:8, 7, 0:48], C["identf"].t[:, 8 * j:8 * j + 8], gts.t[:], True, True, [C["identf"], gts], [pk[7]])
                self.cp(gj.t[:], PS[0:8, 7, 0:48], [pk[7]], [gj])
                for hp in range(2):
                    for p in range(hp * 8, hp * 8 + 8):
                        pl = p - hp * 8
                        col = j * NPG + p
                        self.f.dma(self.f.pool, lambda: nc.gpsimd.indirect_dma_start(out=pgf.t[:], out_offset=None, in_=pool_ap[:, :],
                                   in_offset=bass.IndirectOffsetOnAxis(ap=idx.t[:, col:col + 1], axis=0)), [idx], [pgf], sb=pgf)
                        self.cp(pgb.t[:, 0:512], pgf.t[:, 0:512], [pgf], [pgb], eng="dve")
                        self.cp(pgb.t[:, 512:1024], pgf.t[:, 512:1024], [pgf], [pgb], eng="act")
                        pv4 = pgb.t[:].rearrange("p (k g d) -> p k g d", k=4, g=4)
                        kpair_T(pv4[:, 2], pgb, KsTj.t[:, :, p * 128:(p + 1) * 128], KsTj, 6)
                        pb7 = PS[:, 7, :].bitcast(BF16)
                        for kv in range(2):
                            for g in range(4):
                                self.tr(pb7[0:64, (kv * 4 + g) * 128:(kv * 4 + g + 1) * 128], pv4[:, kv, g, :], idb.t[:], [pgb, idb], [pk[7]])
                        self.cp(kcmj.t[:, :, :, pl * 128:(pl + 1) * 128], pb7[0:64, :].rearrange("p (kv g t) -> p kv g t", kv=2, g=4), [pk[7]], [kcmj], eng="act")
                        self.cp(Vsj.t[:, p, :, 0:64], pv4[:, 3], [pgb], [Vsj], eng="pool")
                    for kv in range(2):
                        for ch in range(2):
                            i4 = kv * 2 + ch
                            b = 2 + i4
                            for l in range(32):
                                self.mm(PS[:, b, 0:128], w1c.t[:, kv, l, ch * 128:(ch + 1) * 128], kcmj.t[:, kv, :, l::32], l == 0, l == 31, [w1c, kcmj], [pk[b]])
                            self.act(xg.t[:, i4, :], PS[:, b, 0:128], AF.Identity, [pk[b], b1c], [xg], bias=b1c.t[:, i4:i4 + 1])
                    self.gelu_to(xg.t[:], ug.t[:], Hgj.t[:], xg, ug, Hgj)
                    for half in range(2):
                        for ch in range(2):
                            self.mm(PS[half * 64:(half + 1) * 64, 6, 0:64], w2c.t[:, 0, ch, :], Hgj.t[:, ch, half * 64:(half + 1) * 64], ch == 0, ch == 1, [w2c, Hgj], [pk[6]])
                    for ch in range(2):
                        self.mm(PS[0:64, 6, 128:256], w2c.t[:, 1, ch, :], Hgj.t[:, 2 + ch, :], ch == 0, ch == 1, [w2c, Hgj], [pk[6]])
                    self.cp(kcTj.t[:, :, hp * 32:(hp + 1) * 32], PS[:, 6, 0:64].rearrange("p (g n) -> p g n", g=2), [pk[6]], [kcTj])
                    self.cp(vcSj.t[:, :, hp * 32:(hp + 1) * 32], PS[0:64, 6, 128:256].rearrange("p (g n) -> p g n", g=4), [pk[6]], [vcSj])
                pb7 = PS[:, 7, :].bitcast(BF16)
                for g in range(4):
                    self.tr(pb7[0:64, g * 64:(g + 1) * 64], vcSj.t[:, g, :], idb.t[0:64, 0:64], [vcSj, idb], [pk[7]])
                self.cp(VCXj.t[:, :, 0:64], pb7[0:64, 0:256].rearrange("p (g d) -> p g d", g=4), [pk[7]], [VCXj])
                e = score_chunk(lambda g, gl: kcTj.t[:, gl, :], kcTj, j, 64)
                pv(e, 64, lambda g: VCXj.t[:, g, :], VCXj, 97, True, True)
                evac(0, 97, True)
                self.tt(tmpi.t[:], oc.t[:, :, 65:97], rec16.t[:].unsqueeze(2).broadcast_to([8, 16, 32]), ALU.mult, [oc, rec16], [tmpi])
                self.f.op(self.f.dve, lambda: nc.vector.tensor_reduce(out=impj.t[:], in_=tmpi.t[:].rearrange("p (g r) b -> p g b r", r=4), axis=AX.X, op=ALU.add), [tmpi], [impj])
                self.cp(imp33.t[:, :, 1:32], impj.t[:, :, 1:32], [impj], [imp33])
                for g in range(4):
                    self.f.op(self.f.dve, lambda: nc.vector.max(out=m8.t[:, 0:8], in_=imp33.t[:, g, :]), [imp33], [m8])
                    self.f.op(self.f.dve, lambda: nc.vector.match_replace(out=wk33.t[:], in_to_replace=m8.t[:, 0:8], in_values=imp33.t[:, g, :], imm_value=-1e9), [imp33, m8], [wk33])
                    self.f.op(self.f.dve, lambda: nc.vector.max(out=m8.t[:, 8:16], in_=wk33.t[:]), [wk33], [m8])
                    self.ts(selbj.t[:, g, :], imp33.t[:, g, 0:32], m8.t[:, 15:16], -NEG, ALU.is_ge, ALU.mult, [imp33, m8], [selbj])
                self.ts(selbj.t[:], selbj.t[:], NEG, None, ALU.add, None, [selbj], [selbj])
                self.cp(selx.t[:], selbj.t[:].unsqueeze(3).broadcast_to([8, 4, 32, 64]), [selbj], [selx], eng="pool")
                for c in range(NPG):
                    e = score_chunk(lambda g, gl: KsTj.t[:, gl, c * 128:(c + 1) * 128], KsTj, j, 128,
                                    per_g_bias=lambda g: selx.t[:, g, 2 * c:2 * c + 2, :].rearrange("p a b -> p (a b)"))
                    pv(e, 128, lambda g: Vsj.t[:, c, g, :], Vsj, 65, c == 0, False)
                e = score_chunk(lambda g, gl: KsTn.t[:, gl, :], KsTn, j, 128, bias_lhsT=C["Bnew0"].t[:, 128 - 8 * j:256 - 8 * j], biast=C["Bnew0"])
                pv(e, 128, lambda g: Vsn.t[:, g, :], Vsn, 65, False, True)
                evac(1, 65, False)
                r0 = (a * NS + j) * 512
                self.load(wbf.t[:], din["winst"][r0:r0 + 512, :].rearrange("(c p) x -> p c x", p=128), wbf)
                self.cp(wbb.t[:, 0:2], wbf.t[:, 0:2], [wbf], [wbb], eng="dve")
                self.cp(wbb.t[:, 2:4], wbf.t[:, 2:4], [wbf], [wbb], eng="pool")
                for c in range(4):
                    wv = wbb.t[:, c, :].rearrange("p (kv g d) -> p kv g d", kv=2, g=4)
                    kpair_T(wv[:, 0], wbb, KwTj.t[:, :, c * 128:(c + 1) * 128], KwTj, 6)
                    self.cp(Vwj.t[:, c, :, 0:64], wv[:, 1], [wbb], [Vwj], eng="pool")
                for c in range(4):
                    if c == 0:
                        e = score_chunk(lambda g, gl: KwTj.t[:, gl, c * 128:(c + 1) * 128], KwTj, j, 128, bias_lhsT=C["BwinS"].t[:], biast=C["BwinS"])
                    else:
                        e = score_chunk(lambda g, gl: KwTj.t[:, gl, c * 128:(c + 1) * 128], KwTj, j, 128)
                    pv(e, 128, lambda g: Vwj.t[:, c, g, :], Vwj, 65, c == 0, False)
                e = score_chunk(lambda g, gl: KwTn.t[:, gl, :], KwTn, j, 128, bias_lhsT=C["Bnew0"].t[:, 128 - 8 * j:256 - 8 * j], biast=C["Bnew0"])
                pv(e, 128, lambda g: Vwn.t[:, g, :], Vwn, 65, False, True)
                evac(2, 65, False)
                self.cp(attjb.t[:], attj.t[:].rearrange("p h d -> p (h d)"), [attj], [attjb])
                pb6 = PS[:, 6, :].bitcast(BF16)
                for c in range(8):
                    self.tr(pb6[:, c * 8:(c + 1) * 8], attjb.t[:, c * 128:(c + 1) * 128], idb.t[0:8, 0:8], [attjb, idb], [pk[6]])
                self.cp(attT.t[:, :, 8 * j:8 * j + 8], pb6[:, 0:64].rearrange("p (c t) -> p c t", c=8), [pk[6]], [attT])
            self.f.barrier()
            self.f.release([pti, pgf, wbf])

    def hg_layer(self, layer):
        r = layer // 2
        cfg = self.cfg
        nc = self.nc
        NT, SEQ, NS = cfg.NT, cfg.SEQ, cfg.NS
        C, PS, pk, din, dout = self.C, self.PS, self.pk, self.din, self.dout
        idb = C["identb"]
        with ExitStack() as st:
            sb = lambda n, shp, dt: self.sb(st, n, shp, dt)
            gmx = self.bcast_row(st, "g_mix", din["norm_mix"][layer, :])
            win = self.load_cast_weight(st, "hwin", din["hg_w_in"][r * D:(r + 1) * D, :], 8, 4 * D)
            wout = self.load_cast_weight(st, "hwout", din["hg_w_out"][r * D:(r + 1) * D, :], 8, D)
            gn = sb("gn", [128, 128], F32)
            self.load(gn.t[:], din["hg_norm"][r, :].partition_broadcast(128), gn)
            lbl = sb("lbl", [16, 128], F32)
            lbT = sb("lbT", [128, 16], F32)
            lb = sb("lb", [128, 8], F32)
            oml = sb("oml", [128, 8], F32)
            self.load(lbl.t[:], din["hg_lb"].rearrange("r (h k) -> (r h) k", k=128), lbl)
            self.tr(PS[:, 7, 0:16], lbl.t[:], C["identf"].t[0:16, 0:16], [lbl, C["identf"]], [pk[7]])
            self.cp(lbT.t[:], PS[:, 7, 0:16], [pk[7]], [lbT])
            if r == 0:
                self.mset(lb.t[:], 0.0, [lb])
            else:
                self.tt(lb.t[:], lbT.t[:, 8:16], lbT.t[:, 0:8], ALU.subtract, [lbT], [lb])
                self.act(lb.t[:], lb.t[:], AF.Sigmoid, [lb], [lb])
            self.ts(oml.t[:], lb.t[:], -1.0, 1.0, ALU.mult, ALU.add, [lb], [oml])
            lb_b = lb.t[:].unsqueeze(2).broadcast_to([128, 8, 128])
            oml_b = oml.t[:].unsqueeze(2).broadcast_to([128, 8, 128])
            S = [sb("S%d" % h, [128, 128], F32) for h in range(8)]
            Sb = [sb("Sb%d" % h, [128, 128], BF16) for h in range(8)]
            for h in range(8):
                self.mset(S[h].t[:], 0.0, [S[h]])
                self.mset(Sb[h].t[:], 0.0, [Sb[h]])
            xt = sb("xt", [128, D], F32)
            hT = sb("hT", [128, 8, 128], BF16)
            f_a = sb("f_a", [128, 8, 128], F32)
            f_k = sb("f_k", [128, 8, 128], F32)
            f_b = sb("f_b", [128, 8, 128], F32)
            f_e = sb("f_e", [128, 8, 128], F32)
            f_n = sb("f_n", [128, 8, 128], F32)
            f_q = sb("f_q", [128, 8, 128], F32)
            qtl = sb("qtl", [128, 8, 128], BF16)
            ktl = sb("ktl", [128, 8, 128], BF16)
            vb = sb("vb", [128, D], BF16)
            gs = sb("gs", [128, D], F32)
            ones = sb("ones", [128, 128], F32)
            self.mset(ones.t[:], 1.0, [ones])
            AmZ = [sb("AmZ%d" % i, [128, 64], BF16) for i in range(4)]
            ktZ = [sb("ktZ%d" % i, [128, 128], BF16) for i in range(4)]
            for i in range(4):
                self.mset(AmZ[i].t[:], 0.0, [AmZ[i]])
                self.mset(ktZ[i].t[:], 0.0, [ktZ[i]])
            mA2 = sb("mA2", [128, 64], F32)
            self.load(mA2.t[0:64, :], self.cdram["maskA"][:, :], mA2)
            self.load(mA2.t[64:128, :], self.cdram["maskA"][:, :], mA2)
            stmp = sb("stmp", [128, 128], F32)
            osb = sb("osb", [128, 8, 128], F32)
            sq = sb("sq", [128, 8, 128], F32)
            ms = sb("ms", [128, 8], F32)
            onb = sb("onb", [128, D], BF16)
            oT = sb("oT", [128, 8, 128], BF16)

            def project(tt):
                self.load(xt.t[:], self.xres[tt * 128:(tt + 1) * 128, :], xt, R=[self.xres_k[tt]])
                self.rms_hT(xt, gmx, hT, 7)
                for kind in range(2):
                    for hd in range(8):
                        b = kind * 2 + hd // 4
                        c0 = (hd % 4) * 128
                        for c in range(8):
                            self.mm(PS[:, b, c0:c0 + 128], win.t[:, c, kind * D + hd * 128:kind * D + (hd + 1) * 128], hT.t[:, c, :],
                                    (c == 0) and (hd % 4 == 0), (c == 7) and (hd % 4 == 3), [win, hT], [pk[b]], skip=True)
                for kind in range(2):
                    for half in range(2):
                        b = 4 + kind * 2 + half
                        n0 = (2 + kind) * D + half * 512
                        for c in range(8):
                            self.mm(PS[:, b, :], hT.t[:, c, :], win.t[:, c, n0:n0 + 512], c == 0, c == 7, [hT, win], [pk[b]])
                for half in range(2):
                    fv = f_a.t[:, half * 4:(half + 1) * 4, :].rearrange("p h t -> p (h t)")
                    self.act(fv, PS[:, 2 + half, :], AF.Sigmoid, [pk[2 + half]], [f_a])
                    qv = f_q.t[:, half * 4:(half + 1) * 4, :].rearrange("p h t -> p (h t)")
                    self.act(qv, PS[:, half, :], AF.Silu, [pk[half]], [f_q])
                    self.cp(vb.t[:, half * 512:(half + 1) * 512], PS[:, 4 + half, :], [pk[4 + half]], [vb])
                    self.act(gs.t[:, half * 512:(half + 1) * 512], PS[:, 6 + half, :], AF.Silu, [pk[6 + half]], [gs])
                self.ts(f_k.t[:], f_a.t[:], -1.0, 1.0, ALU.mult, ALU.add, [f_a], [f_k], eng="pool")
                self.tt(f_k.t[:], f_k.t[:], oml_b, ALU.mult, [f_k, oml], [f_k], eng="pool")
                self.tt(f_a.t[:], f_a.t[:], oml_b, ALU.mult, [f_a, oml], [f_a])
                self.tt(f_a.t[:], f_a.t[:], lb_b, ALU.add, [f_a, lb], [f_a])
                self.act(f_a.t[:], f_a.t[:], AF.Ln, [f_a], [f_a])

            def finish_elem():
                self.act(f_e.t[:], f_b.t[:], AF.Exp, [f_b], [f_e])
                self.act(f_n.t[:], f_b.t[:], AF.Exp, [f_b], [f_n], scale=-1.0)
                self.stt(qtl.t[:], f_q.t[:], 128.0 ** -0.5, f_e.t[:], ALU.mult, ALU.mult, [f_q, f_e], [qtl])
                self.tt(ktl.t[:], f_k.t[:], f_n.t[:], ALU.mult, [f_k, f_n], [ktl], eng="pool")

            def post_o(tt):
                for half in range(2):
                    self.cp(osb.t[:, half * 4:(half + 1) * 4, :].rearrange("p h d -> p (h d)"), PS[:, 4 + half, :], [pk[4 + half]], [osb], eng="act")
                self.tt(sq.t[:], osb.t[:], osb.t[:], ALU.mult, [osb], [sq], eng="pool")
                self.f.op(self.f.dve, lambda: nc.vector.tensor_reduce(out=ms.t[:], in_=sq.t[:], axis=AX.X, op=ALU.add), [sq], [ms])
                self.act(ms.t[:], ms.t[:], AF.Sqrt, [ms], [ms], scale=1.0 / 128, bias=1e-6)
                self.f.op(self.f.dve, lambda: nc.vector.reciprocal(out=ms.t[:], in_=ms.t[:]), [ms], [ms])
                self.tt(osb.t[:], osb.t[:], ms.t[:].unsqueeze(2).broadcast_to([128, 8, 128]), ALU.mult, [osb, ms], [osb])
                self.tt(osb.t[:], osb.t[:], gn.t[:].unsqueeze(1).broadcast_to([128, 8, 128]), ALU.mult, [osb, gn], [osb], eng="pool")
                self.tt(onb.t[:], osb.t[:].rearrange("p h d -> p (h d)"), gs.t[:], ALU.mult, [osb, gs], [onb])
                self.transpose_to(onb.t, onb, oT, 6)
                for half in range(2):
                    b = 6 + half
                    for c in range(8):
                        self.mm(PS[:, b, :], oT.t[:, c, :], wout.t[:, c, half * 512:(half + 1) * 512], c == 0, c == 7, [oT, wout], [pk[b]])
                    self.tt(xt.t[:, half * 512:(half + 1) * 512], xt.t[:, half * 512:(half + 1) * 512], PS[:, b, :], ALU.add, [xt, pk[b]], [xt])
                self.store(self.xres[tt * 128:(tt + 1) * 128, :], xt.t[:], xt, W=[self.xres_k[tt]])

            qh = sb("qh", [128, 8, 128], BF16)
            kS = sb("kS", [128, 8, 128], BF16)
            targ = [sb("targ%d" % i, [128, 8, 64], F32) for i in range(2)]
            khat = [sb("khat%d" % i, [128, 8, 64], BF16) for i in range(2)]
            AmC = [sb("AmC%d" % i, [128, 8, 64], BF16) for i in range(2)]
            for i in range(2):
                self.mset(AmC[i].t[:], 0.0, [AmC[i]])
            tcnt = [0]
            for tt in range(NT):
                project(tt)
                for hd in range(8):
                    for ck in range(2):
                        cs_ = slice(ck * 64, ck * 64 + 64)
                        self.f.op(self.f.dve, lambda: nc.vector.tensor_tensor_scan(out=f_b.t[:, hd, cs_], data0=ones.t[:, 0:64], data1=f_a.t[:, hd, cs_], initial=0.0, op0=ALU.mult, op1=ALU.add), [ones, f_a], [f_b])
                self.act(f_e.t[:], f_b.t[:], AF.Exp, [f_b], [f_e])
                self.stt(qtl.t[:], f_q.t[:], 128.0 ** -0.5, f_e.t[:], ALU.mult, ALU.mult, [f_q, f_e], [qtl])
                for ck in range(2):
                    cs_ = slice(ck * 64, ck * 64 + 64)
                    ps_ = slice(ck * 64, ck * 64 + 64)
                    bck = f_b.t[:, :, cs_]
                    ta = targ[tcnt[0] % 2]
                    tcnt[0] += 1
                    self.tt(ta.t[:].rearrange("p h (i u) -> p h i u", u=16), bck.rearrange("p h (i u) -> p h i u", u=16),
                            f_b.t[:, :, ck * 64:ck * 64 + 64:16].unsqueeze(3).broadcast_to([128, 8, 4, 16]), ALU.subtract, [f_b], [ta])
                    self.act(ta.t[:], ta.t[:], AF.Exp, [ta], [ta])
                    self.stt(qh.t[:, :, cs_], f_q.t[:, :, cs_], 128.0 ** -0.5, ta.t[:], ALU.mult, ALU.mult, [f_q, ta], [qh])
                    ta = targ[tcnt[0] % 2]
                    tcnt[0] += 1
                    self.tt(ta.t[:], bck, f_b.t[:, :, ck * 64 + 63:ck * 64 + 64].broadcast_to([128, 8, 64]), ALU.subtract, [f_b], [ta])
                    self.act(ta.t[:], ta.t[:], AF.Exp, [ta], [ta], scale=-1.0)
                    self.tt(kS.t[:, :, cs_], f_k.t[:, :, cs_], ta.t[:], ALU.mult, [f_k, ta], [kS], eng="pool")
                    for I in range(4):
                        ta = targ[tcnt[0] % 2]
                        kh = khat[tcnt[0] % 2]
                        tcnt[0] += 1
                        c0 = ck * 64 + 16 * I
                        self.tt(ta.t[:], bck, f_b.t[:, :, c0:c0 + 1].broadcast_to([128, 8, 64]), ALU.subtract, [f_b], [ta])
                        self.ts(ta.t[:], ta.t[:], -80.0, None, ALU.max, None, [ta], [ta])
                        self.act(ta.t[:], ta.t[:], AF.Exp, [ta], [ta], scale=-1.0)
                        self.tt(kh.t[:], f_k.t[:, :, cs_], ta.t[:], ALU.mult, [f_k, ta], [kh], eng="pool")
                        for hd in range(8):
                            self.mm(PS[ps_, 3, hd * 64 + 16 * I:hd * 64 + 16 * I + 16], kh.t[:, hd, :], qh.t[:, hd, c0:c0 + 16], True, True, [kh, qh], [pk[3]])
                    self.tt(AmC[ck].t[ps_, :, :], PS[ps_, 3, :].rearrange("p (h t) -> p h t", h=8), mA2.t[ps_, :].unsqueeze(1).broadcast_to([64, 8, 64]), ALU.mult, [pk[3], mA2], [AmC[ck]])
                    for hd in range(8):
                        z = hd % 2
                        zz = ck * 2 + z
                        ob = 4 + hd // 4
                        oc0 = (hd % 4) * 128
                        pb = PS[:, 2, :].bitcast(BF16)
                        self.tr(pb[ps_, z * 128:(z + 1) * 128], kS.t[:, hd, cs_], idb.t[:], [kS, idb], [pk[2]])
                        self.cp(ktZ[zz].t[ps_, :], pb[ps_, z * 128:(z + 1) * 128], [pk[2]], [ktZ[zz]], eng="act")
                        first = (hd % 4 == 0)
                        self.mm(PS[ps_, ob, oc0:oc0 + 128], qtl.t[:, hd, cs_], Sb[hd].t[:], first, False, [qtl, Sb[hd]], [pk[ob]], skip=True)
                        self.mm(PS[ps_, ob, oc0:oc0 + 128], AmC[ck].t[:, hd, :], vb.t[:, hd * 128:(hd + 1) * 128], False, (hd % 4 == 3), [AmC[ck], vb], [pk[ob]], skip=True)
                        self.mm(PS[:, z, 0:128], ktZ[zz].t[:], vb.t[:, hd * 128:(hd + 1) * 128], True, True, [ktZ[zz], vb], [pk[z]])
                        self.stt(S[hd].t[:], S[hd].t[:], f_e.t[:, hd, ck * 64 + 63:ck * 64 + 64], PS[:, z, 0:128], ALU.mult, ALU.add, [S[hd], f_e, pk[z]], [S[hd]])
                        self.cp(Sb[hd].t[:], S[hd].t[:], [S[hd]], [Sb[hd]], eng="pool")
                post_o(tt)
            for hd in range(8):
                r0 = (r * 8 + hd) * 128
                self.store(dout["nh_p"][r0:r0 + 128, :], S[hd].t[:], S[hd], W=[self.outk])
            project(NT)
            for hd in range(8):
                for j in range(NS):
                    cs_ = slice(8 * j, 8 * j + 8)
                    self.f.op(self.f.dve, lambda: nc.vector.tensor_tensor_scan(out=f_b.t[:, hd, cs_], data0=ones.t[:, 0:8], data1=f_a.t[:, hd, cs_], initial=0.0, op0=ALU.mult, op1=ALU.add), [ones, f_a], [f_b])
            finish_elem()
            S0f = [sb("S0f%d" % i, [128, 128], F32) for i in range(2)]
            S0b = [sb("S0b%d" % i, [128, 128], BF16) for i in range(2)]
            ktok = sb("ktok", [128, 128], BF16)
            ktm = [sb("ktm%d" % i, [128, 128], BF16) for i in range(2)]
            Am = sb("Am", [128, 128], BF16)
            oiT = sb("oiT", [128, 128], F32)
            Sn = [sb("Sn%d" % i, [128, 128], F32) for i in range(2)]
            cnt = 0
            for hd in range(8):
                ob = 4 + hd // 4
                oc0 = (hd % 4) * 128
                pb = PS[:, 2, :].bitcast(BF16)
                self.tr(pb[:, 0:128], ktl.t[:, hd, :], idb.t[:], [ktl, idb], [pk[2]])
                self.cp(ktok.t[:], pb[:, 0:128], [pk[2]], [ktok], eng="act")
                self.mm(PS[:, 3, 0:128], ktl.t[:, hd, :], qtl.t[:, hd, :], True, True, [ktl, qtl], [pk[3]])
                self.tt(Am.t[:], PS[:, 3, 0:128], C["maskS"].t[:], ALU.mult, [pk[3], C["maskS"]], [Am])
                for j in range(NS):
                    z = cnt % 2
                    cnt += 1
                    r0 = ((r * NS + j) * 8 + hd) * 128
                    self.load(S0f[z].t[:], din["hst"][r0:r0 + 128, :], S0f[z])
                    self.cp(S0b[z].t[:], S0f[z].t[:], [S0f[z]], [S0b[z]], eng="pool")
                    self.mm(PS[:, 0, 8 * j:8 * j + 8], S0b[z].t[:], qtl.t[:, hd, 8 * j:8 * j + 8], j == 0, j == NS - 1, [S0b[z], qtl], [pk[0]], skip=True)
                    self.ts(ktm[z].t[:], ktok.t[:], C["rowsel"].t[:, j:j + 1], None, ALU.mult, None, [ktok, C["rowsel"]], [ktm[z]], eng="pool")
                    self.mm(PS[:, 1, z * 128:(z + 1) * 128], ktm[z].t[:], vb.t[:, hd * 128:(hd + 1) * 128], True, True, [ktm[z], vb], [pk[1]])
                    self.tt(Sn[z].t[:], S0f[z].t[:], PS[:, 1, z * 128:(z + 1) * 128], ALU.add, [S0f[z], pk[1]], [Sn[z]])
                    self.ts(Sn[z].t[:], Sn[z].t[:], f_e.t[:, hd, 8 * j + 7:8 * j + 8], None, ALU.mult, None, [Sn[z], f_e], [Sn[z]])
                    self.store(dout["nh_s"][r0:r0 + 128, :], Sn[z].t[:], Sn[z], W=[self.outk])
                self.cp(oiT.t[:], PS[:, 0, 0:128], [pk[0]], [oiT], eng="act")
                first = hd % 4 == 0
                self.mm(PS[:, ob, oc0:oc0 + 128], Am.t[:], vb.t[:, hd * 128:(hd + 1) * 128], first, False, [Am, vb], [pk[ob]], skip=True)
                self.mm(PS[:, ob, oc0:oc0 + 128], oiT.t[:], C["identf"].t[:], False, hd % 4 == 3, [oiT, C["identf"]], [pk[ob]], skip=True)
            post_o(NT)
            self.f.barrier()
            self.f.release([gmx, gn, lbl, xt, mA2] + S + S0f + Sn)


def make_in_maps(cfg, inputs, n_cores=8):
    hc = host_consts(cfg)
    NS = cfg.NS
    maps = []
    f32 = np.float32
    A = lambda a: np.ascontiguousarray(np.asarray(a))
    nb = inputs["x_prompt"].shape[0]
    pool = A(inputs["cache_nsa_kv"]).reshape(2 * cfg.NPHYS * 128, 1024)
    for c in range(n_cores):
        b = c % nb
        sl = slice(c * NS, (c + 1) * NS)
        m = {
            "xp": A(inputs["x_prompt"][b]),
            "xs": A(inputs["x_sample"][sl]).reshape(128, D),
            "pool": pool,
            "winst": A(inputs["state_nsa_win"][:, sl]).reshape(2 * NS * 512, 512),
            "hst": A(inputs["state_hgrn"][:, sl]).reshape(2 * NS * 8 * 128, 128),
            "ptab": A(inputs["page_table"][sl]).reshape(1, NS * cfg.NPG).astype(np.int32),
            "pp": A(inputs["p_prompt"][:, b]).reshape(4 * cfg.SEQ, PLE),
            "psm": A(inputs["p_sample"][:, sl]).reshape(4 * 128, PLE),
            "norm_mix": A(inputs["norm_mix"]), "norm_mlp": A(inputs["norm_mlp"]), "norm_ple": A(inputs["norm_ple"]),
            "norm_final": A(inputs["norm_final"]).reshape(1, D),
            "nsa_w_in": A(inputs["nsa_w_in"]).reshape(2 * D, NSA_IN),
            "cmp_pe": A(inputs["nsa_cmp_pe"]).reshape(128, 64),
            "cmp_w1": A(inputs["nsa_cmp_w1"]).reshape(2 * 2 * 32 * 64, 256),
            "cmp_w2": A(inputs["nsa_cmp_w2"]).reshape(2 * 2 * 256, 64),
            "nsa_w_out": A(inputs["nsa_w_out"]).reshape(2 * D, D),
            "hg_w_in": A(inputs["hg_w_in"]).reshape(2 * D, 4 * D),
            "hg_lb": A(inputs["hg_lb_logits"]), "hg_norm": A(inputs["hg_norm"]),
            "hg_w_out": A(inputs["hg_w_out"]).reshape(2 * D, D),
            "mlp_w1": A(inputs["mlp_w1"]).reshape(4 * D, DFF),
            "mlp_w2": A(inputs["mlp_w2"]).reshape(4 * DFF, D),
            "ple_proj": A(inputs["ple_w_proj"]).reshape(4 * PLE, D),
            "ple_gate": A(inputs["ple_w_gate"]).reshape(4 * D, D),
        }
        for k, v in hc.items():
            m["c_" + k] = v
        maps.append(m)
    return maps


def assemble(cfg, res, nb=4, n_cores=8):
    NS = cfg.NS
    SEQ = cfg.SEQ
    R = res
    y_p = np.stack([R[b]["y_p"] for b in range(nb)])
    y_s = np.concatenate([R[c]["y_s"].reshape(NS, 8, D) for c in range(n_cores)])
    nkv_p = np.stack([R[b]["nkv_p"].reshape(2, SEQ, 4, G, HD) for b in range(nb)], axis=1)
    nkv_s = np.concatenate([R[c]["nkv_s"].reshape(2, NS, 8, 4, G, HD) for c in range(n_cores)], axis=1)
    nwin_p = np.stack([R[b]["nwin_p"].reshape(2, 512, 2, G, HD) for b in range(nb)], axis=1)
    nwin_s = np.concatenate([R[c]["nwin_s"].reshape(2, NS, 512, 2, G, HD) for c in range(n_cores)], axis=1)
    nh_p = np.stack([R[b]["nh_p"].reshape(2, 8, 128, 128) for b in range(nb)], axis=1)
    nh_s = np.concatenate([R[c]["nh_s"].reshape(2, NS, 8, 128, 128) for c in range(n_cores)], axis=1)
    return (y_p, y_s, nkv_p, nkv_s, nwin_p, nwin_s, nh_p, nh_s)


_CACHE = {}
_USED = {}


def run(cfg, inputs, stages=("nsa", "hg", "post")):
    key = (cfg.SEQ, cfg.NS, cfg.NPG, cfg.NPHYS, cfg.depth, tuple(stages))
    if key not in _CACHE:
        p = Prog(cfg, stages)
        _CACHE[key] = p.build()
        _USED[key] = list(p.din.keys())
    nc = _CACHE[key]
    maps = make_in_maps(cfg, inputs)
    used = set(_USED[key])
    maps = [{k: v for k, v in m.items() if k in used} for m in maps]
    res = run_bass_kernel_spmd(nc, maps, core_ids=list(range(8)))
    return assemble(cfg, res.results, nb=inputs["x_prompt"].shape[0])


def kernel(**inputs):
    cfg = Cfg(SEQ=int(inputs["x_prompt"].shape[1]), NS=16, NPG=int(inputs["page_table"].shape[1]),
              NPHYS=int(inputs["cache_nsa_kv"].shape[1]), depth=4)
    return run(cfg, inputs)
```
